# Optimizing a Trainium2 kernel written in Bass

```python
import jax, jax.numpy as jnp
from jax import lax
import numpy as np

D_MODEL = 1024
BATCH = 8
SEQ = 2048
DEPTH = 1

N_META = 16
Q_BLOCK = 128
MLA_HEADS = 8
MLA_Q_RANK = 384
MLA_KV_RANK = 128
MLA_NOPE_DIM = 64
MLA_ROPE_DIM = 32
MLA_QK_DIM = MLA_NOPE_DIM + MLA_ROPE_DIM
MLA_V_DIM = 64
MLA_V_WIDTH = MLA_HEADS * MLA_V_DIM
ROPE_THETA = 10000.0
FOX_HEADS = 8
FOX_HEAD_DIM = 64
FOX_WIDTH = FOX_HEADS * FOX_HEAD_DIM
D_FF = 2816
CONV_WIDTH = 3
LN_EPS = 1e-5
RMS_EPS = 1e-6
DN_ALPHA = (2 * DEPTH) ** 0.25
DN_BETA = (8 * DEPTH) ** -0.25
NEG_INF = -1e30
IN_SPLITS = (MLA_Q_RANK, MLA_KV_RANK, MLA_ROPE_DIM, FOX_WIDTH, FOX_WIDTH, FOX_WIDTH, FOX_HEADS, 2 * D_MODEL)
IN_TOTAL = MLA_Q_RANK + MLA_KV_RANK + MLA_ROPE_DIM + 3 * FOX_WIDTH + FOX_HEADS + 2 * D_MODEL

kernel_name = 'hybrid_mla_fox_convglu_deepnorm_meta'


def layer_norm(x, g, b):
    xf = x.astype(jnp.float32)
    mu = jnp.mean(xf, axis=-1, keepdims=True)
    var = jnp.mean(jnp.square(xf - mu), axis=-1, keepdims=True)
    y = (xf - mu) * lax.rsqrt(var + LN_EPS)
    return (y * g.astype(jnp.float32) + b.astype(jnp.float32)).astype(x.dtype)


def rms_norm(x, g):
    xf = x.astype(jnp.float32)
    y = xf * lax.rsqrt(jnp.mean(jnp.square(xf), axis=-1, keepdims=True) + RMS_EPS)
    return (y * g.astype(jnp.float32)).astype(x.dtype)


def apply_rope(t, pos):
    half = t.shape[-1] // 2
    inv_freq = ROPE_THETA ** (-jnp.arange(half, dtype=jnp.float32) / half)
    ang = pos.astype(jnp.float32)[:, None] * inv_freq[None, :]
    cos = jnp.cos(ang).astype(t.dtype)
    sin = jnp.sin(ang).astype(t.dtype)
    t1, t2 = t[..., :half], t[..., half:]
    return jnp.concatenate([t1 * cos - t2 * sin, t2 * cos + t1 * sin], axis=-1)


def causal_block_attention(q, k, v, scale, cum_logf=None):
    B, H, L, dk = q.shape
    dv = v.shape[-1]
    pad = (-N_META) % Q_BLOCK
    Lp = L + pad
    nb = Lp // Q_BLOCK
    pad4 = lambda a: jnp.pad(a, ((0, 0), (0, 0), (pad, 0), (0, 0)))
    q, k, v = pad4(q), pad4(k), pad4(v)
    kpos = jnp.arange(Lp)
    key_valid = kpos >= pad
    qb = q.reshape(B, H, nb, Q_BLOCK, dk).transpose(2, 0, 1, 3, 4)
    idx = jnp.arange(nb)
    if cum_logf is None:
        c = None
        xs = (idx, qb)
    else:
        c = jnp.pad(cum_logf, ((0, 0), (0, 0), (pad, 0)))
        cb = c.reshape(B, H, nb, Q_BLOCK).transpose(2, 0, 1, 3)
        xs = (idx, qb, cb)

    def one_block(blk):
        i, q_i = blk[0], blk[1]
        s = jnp.einsum('bhqd,bhkd->bhqk', q_i, k).astype(jnp.float32) * scale
        if c is not None:
            s = s + (blk[2][..., :, None] - c[:, :, None, :])
        qpos = i * Q_BLOCK + jnp.arange(Q_BLOCK)
        mask = (kpos[None, :] <= qpos[:, None]) & key_valid[None, :]
        s = jnp.where(mask[None, None], s, NEG_INF)
        p = jax.nn.softmax(s, axis=-1)
        return jnp.einsum('bhqk,bhkd->bhqd', p.astype(v.dtype), v)

    ob = lax.map(one_block, xs)
    o = ob.transpose(1, 2, 0, 3, 4).reshape(B, H, Lp, dv)
    return o[:, :, pad:, :]


def hybrid_mixer(h, w_in, b_gate, b_forget, q_norm_g, w_q_up, kv_norm_g, w_kv_up,
                 w_branch_mla, w_branch_fox, w_out):
    B, L, _ = h.shape
    proj = h @ w_in
    offs = np.cumsum(IN_SPLITS)[:-1].tolist()
    q_lat, kv_lat, k_rope, fq, fk, fv, f_logit, gate_logit = jnp.split(proj, offs, axis=-1)
    pos = jnp.arange(L)

    q = (rms_norm(q_lat, q_norm_g) @ w_q_up).reshape(B, L, MLA_HEADS, MLA_QK_DIM).transpose(0, 2, 1, 3)
    q_nope, q_pe = q[..., :MLA_NOPE_DIM], q[..., MLA_NOPE_DIM:]
    kv = (rms_norm(kv_lat, kv_norm_g) @ w_kv_up).reshape(B, L, MLA_HEADS, MLA_NOPE_DIM + MLA_V_DIM).transpose(0, 2, 1, 3)
    k_nope, v_mla = kv[..., :MLA_NOPE_DIM], kv[..., MLA_NOPE_DIM:]
    q_pe = apply_rope(q_pe, pos)
    k_pe = apply_rope(k_rope[:, None], pos)
    q_mla = jnp.concatenate([q_nope, q_pe], axis=-1)
    k_mla = jnp.concatenate([k_nope, jnp.broadcast_to(k_pe, (B, MLA_HEADS, L, MLA_ROPE_DIM))], axis=-1)
    o_mla = causal_block_attention(q_mla, k_mla, v_mla, MLA_QK_DIM ** -0.5)
    o_mla = o_mla.transpose(0, 2, 1, 3).reshape(B, L, MLA_V_WIDTH)

    heads = lambda t: t.reshape(B, L, FOX_HEADS, FOX_HEAD_DIM).transpose(0, 2, 1, 3)
    log_f = jax.nn.log_sigmoid((f_logit + b_forget).astype(jnp.float32))
    cum = lax.cumsum(log_f, axis=1).transpose(0, 2, 1)
    o_fox = causal_block_attention(heads(fq), heads(fk), heads(fv), FOX_HEAD_DIM ** -0.5, cum)
    o_fox = o_fox.transpose(0, 2, 1, 3).reshape(B, L, FOX_WIDTH)

    gates = jax.nn.sigmoid(gate_logit + b_gate)
    g_mla, g_fox = gates[..., :D_MODEL], gates[..., D_MODEL:]
    merged = g_mla * (o_mla @ w_branch_mla) + g_fox * (o_fox @ w_branch_fox)
    return merged @ w_out


def causal_depthwise_conv(x, w, b):
    C = x.shape[-1]
    y = lax.conv_general_dilated(x, w[:, None, :].astype(x.dtype), window_strides=(1,),
                                 padding=[(CONV_WIDTH - 1, 0)],
                                 dimension_numbers=('NWC', 'WIO', 'NWC'),
                                 feature_group_count=C)
    return y + b


def conv_glu_ffn(h, w_up, conv_w, conv_b, w_down):
    up = h @ w_up
    gate, val = up[..., :D_FF], up[..., D_FF:]
    gate = causal_depthwise_conv(gate, conv_w, conv_b)
    return (jax.nn.silu(gate) * val) @ w_down


def setup_inputs(seed: int = 0) -> dict:
    key = jax.random.key(seed)
    ks = jax.random.split(key, 24)
    f32 = jnp.float32
    nrm = lambda k, shape, s: jax.random.normal(k, shape, f32) * s
    gain = lambda k, shape: 1.0 + 0.05 * jax.random.normal(k, shape, f32)
    return {
        'x': nrm(ks[0], (BATCH, SEQ, D_MODEL), 1.0),
        'meta_tokens': nrm(ks[1], (N_META, D_MODEL), 1.0),
        'ln_emb_g': gain(ks[2], (D_MODEL,)),
        'ln_emb_b': nrm(ks[3], (D_MODEL,), 0.02),
        'w_in': nrm(ks[4], (DEPTH, D_MODEL, IN_TOTAL), D_MODEL ** -0.5),
        'b_gate': nrm(ks[5], (DEPTH, 2 * D_MODEL), 0.02),
        'b_forget': jax.random.uniform(ks[6], (DEPTH, FOX_HEADS), f32, 2.0, 6.0),
        'q_norm_g': gain(ks[7], (DEPTH, MLA_Q_RANK)),
        'w_q_up': nrm(ks[8], (DEPTH, MLA_Q_RANK, MLA_HEADS * MLA_QK_DIM), MLA_Q_RANK ** -0.5),
        'kv_norm_g': gain(ks[9], (DEPTH, MLA_KV_RANK)),
        'w_kv_up': nrm(ks[10], (DEPTH, MLA_KV_RANK, MLA_HEADS * (MLA_NOPE_DIM + MLA_V_DIM)), MLA_KV_RANK ** -0.5),
        'w_branch_mla': nrm(ks[11], (DEPTH, MLA_V_WIDTH, D_MODEL), MLA_V_WIDTH ** -0.5 * DN_BETA),
        'w_branch_fox': nrm(ks[12], (DEPTH, FOX_WIDTH, D_MODEL), FOX_WIDTH ** -0.5 * DN_BETA),
        'w_out': nrm(ks[13], (DEPTH, D_MODEL, D_MODEL), D_MODEL ** -0.5 * DN_BETA),
        'ln_mix_g': gain(ks[14], (DEPTH, D_MODEL)),
        'ln_mix_b': nrm(ks[15], (DEPTH, D_MODEL), 0.02),
        'w_ffn_up': nrm(ks[16], (DEPTH, D_MODEL, 2 * D_FF), D_MODEL ** -0.5 * DN_BETA),
        'conv_w': nrm(ks[17], (DEPTH, CONV_WIDTH, D_FF), CONV_WIDTH ** -0.5),
        'conv_b': nrm(ks[18], (DEPTH, D_FF), 0.02),
        'w_ffn_down': nrm(ks[19], (DEPTH, D_FF, D_MODEL), D_FF ** -0.5 * DN_BETA),
        'ln_ffn_g': gain(ks[20], (DEPTH, D_MODEL)),
        'ln_ffn_b': nrm(ks[21], (DEPTH, D_MODEL), 0.02),
    }


def reference(x, meta_tokens, ln_emb_g, ln_emb_b, w_in, b_gate, b_forget, q_norm_g, w_q_up,
              kv_norm_g, w_kv_up, w_branch_mla, w_branch_fox, w_out, ln_mix_g, ln_mix_b,
              w_ffn_up, conv_w, conv_b, w_ffn_down, ln_ffn_g, ln_ffn_b):
    B = x.shape[0]
    meta = jnp.broadcast_to(meta_tokens[None].astype(x.dtype), (B, N_META, D_MODEL))
    h = layer_norm(jnp.concatenate([meta, x], axis=1), ln_emb_g, ln_emb_b)
    for l in range(DEPTH):
        m = hybrid_mixer(h, w_in[l], b_gate[l], b_forget[l], q_norm_g[l], w_q_up[l], kv_norm_g[l],
                         w_kv_up[l], w_branch_mla[l], w_branch_fox[l], w_out[l])
        h = layer_norm(DN_ALPHA * h + m, ln_mix_g[l], ln_mix_b[l])
        f = conv_glu_ffn(h, w_ffn_up[l], conv_w[l], conv_b[l], w_ffn_down[l])
        h = layer_norm(DN_ALPHA * h + f, ln_ffn_g[l], ln_ffn_b[l])
    return h[:, N_META:, :]
```

```python
import numpy as np
import concourse.bass as bass
import concourse.mybir as mybir
from concourse.bass_utils import run_bass_kernel_spmd

F32 = mybir.dt.float32
BF16 = mybir.dt.bfloat16
AF = mybir.ActivationFunctionType
ALU = mybir.AluOpType
AX = mybir.AxisListType


class _Op:
    __slots__ = ("eng", "stream", "pos", "fn", "waits", "clock", "signal", "sigval", "dma")

    def __init__(self, eng, stream, pos, fn, dma):
        self.eng = eng
        self.stream = stream
        self.pos = pos
        self.fn = fn
        self.waits = []
        self.clock = None
        self.signal = dma
        self.sigval = 0
        self.dma = dma


class Sched:
    ENGS = ("pe", "act", "dve", "pool", "sp")
    G = 128

    def __init__(self, nc):
        self.nc = nc
        self.eops = {e: [] for e in self.ENGS}
        self.cstream = {e: [] for e in self.ENGS}
        self.dstream = {}
        self.eclock = {e: {} for e in self.ENGS}
        self.state = {}
        self._gcache = {}

    def _keys(self, ap):
        t = ap.tensor
        name = t.name
        dims = [tuple(d) for d in ap.ap]
        ck = (name, ap.offset, tuple(dims), str(ap.dtype))
        r = self._gcache.get(ck)
        if r is not None:
            return r
        space = str(ap.space)
        if "PSUM" in space.upper():
            pstride, pcount = dims[0]
            p0 = ap.offset // pstride if pstride > 0 else 0
            r = tuple(("PSUM", name, q) for q in range(p0 // 32, (p0 + pcount - 1) // 32 + 1))
            self._gcache[ck] = r
            return r
        n = ap.size()
        esz = ap.nbytes() // max(1, n) if n else 4
        if esz == 0:
            esz = 4
        isdram = "DRAM" in space.upper() or "HBM" in space.upper()
        if isdram:
            fdims = dims
            foff = ap.offset
            qs = (0,)
            G = 4096
        else:
            pstride, pcount = dims[0]
            fdims = dims[1:]
            p0 = ap.offset // pstride if pstride > 0 else 0
            foff = ap.offset - p0 * pstride
            qs = tuple(range(p0 // 32, (p0 + pcount - 1) // 32 + 1))
            G = self.G
        fdims = [d for d in fdims if d[1] > 1 and d[0] != 0]
        if not fdims:
            ranges = [(foff, foff + 1)]
        else:
            fd = sorted(fdims, key=lambda d: abs(d[0]))
            s0, n0 = fd[0]
            outer = fd[1:]
            cnt = 1
            for d in outer:
                cnt *= d[1]
            if cnt > 256:
                lo = foff
                hi = foff + sum((d[1] - 1) * d[0] for d in fd) + 1
                ranges = [(lo, hi)]
            else:
                starts = [foff]
                for (s, c) in outer:
                    starts = [b + i * s for b in starts for i in range(c)]
                ln = (n0 - 1) * s0 + 1
                ranges = [(b, b + ln) for b in starts]
        gs = set()
        for lo, hi in ranges:
            for g in range((lo * esz) // G, (hi * esz - 1) // G + 1):
                gs.add(g)
        r = tuple((name, q, g) for q in qs for g in gs)
        self._gcache[ck] = r
        return r

    def _add(self, eng, fn, reads, writes, dma_key=None):
        dma = dma_key is not None
        if dma:
            lst = self.dstream.setdefault(dma_key, [])
            stream = "d:" + dma_key
            op = _Op(eng, stream, len(lst) + 1, fn, True)
        else:
            lst = self.cstream[eng]
            stream = eng
            op = _Op(eng, stream, len(lst) + 1, fn, False)
        deps = set()
        if dma and lst:
            deps.add(lst[-1])
        rk = []
        wk = []
        pk = []
        for ap in reads:
            for k in self._keys(ap):
                (pk if k[0] == "PSUM" else rk).append(k)
        for ap in writes:
            wk.extend(self._keys(ap))
        st = self.state
        for k in rk:
            s = st.get(k)
            if s is not None and s[0] is not None:
                deps.add(s[0])
        for k in pk:
            s = st.get(k)
            if s is not None:
                if s[0] is not None:
                    deps.add(s[0])
                for rs_, ro_ in s[1].items():
                    if rs_ != stream:
                        deps.add(ro_)
        for k in wk:
            s = st.get(k)
            if s is not None:
                if s[0] is not None:
                    deps.add(s[0])
                deps.update(s[1].values())
        clk = self.eclock[eng]
        for d in sorted(deps, key=lambda d: -d.pos):
            if d.stream == "pe" and eng == "pe" and not dma:
                continue
            if clk.get(d.stream, 0) >= d.pos:
                continue
            op.waits.append((d.stream, d.pos))
            d.signal = True
            for s, p in d.clock.items():
                if clk.get(s, 0) < p:
                    clk[s] = p
        c = dict(clk)
        c[stream] = op.pos
        op.clock = c
        for k in rk + pk:
            s = st.get(k)
            if s is None:
                st[k] = [None, {stream: op}]
            else:
                s[1][stream] = op
        for k in wk:
            st[k] = [op, {}]
        lst.append(op)
        self.eops[eng].append(op)
        return op

    def pe(self, fn, r=(), w=()):
        return self._add("pe", fn, r, w)

    def act(self, fn, r=(), w=()):
        return self._add("act", fn, r, w)

    def dve(self, fn, r=(), w=()):
        return self._add("dve", fn, r, w)

    def pool(self, fn, r=(), w=()):
        return self._add("pool", fn, r, w)

    def op(self, eng, name, *args, **kw):
        wr = [kw[k] for k in ("out", "accum_out") if isinstance(kw.get(k), bass.AP)]
        rd = [v for k, v in kw.items() if k not in ("out", "accum_out") and isinstance(v, bass.AP)]
        if name == "memset":
            wr = [args[0]]
        return self._add(eng, lambda e: getattr(e, name)(*args, **kw), rd, wr)

    def dma(self, eng, key, out, in_, **kw):
        return self._add(eng, lambda e: e.dma_start(out=out, in_=in_, **kw), [in_], [out], dma_key=key)

    def barrier_all(self, aps_by_eng=None):
        last = {}
        for e in self.ENGS:
            if self.cstream[e]:
                last[e] = self.cstream[e][-1]
        for k, lst in self.dstream.items():
            if lst:
                last["d:" + k] = lst[-1]
        for e in ("pe", "act", "dve", "pool", "sp"):
            clk = self.eclock[e]
            waits = []
            for s, d in last.items():
                if clk.get(s, 0) >= d.pos:
                    continue
                waits.append((s, d.pos))
                d.signal = True
                for s2, p in d.clock.items():
                    if clk.get(s2, 0) < p:
                        clk[s2] = p
            if waits:
                op = _Op(e, "nop", 0, None, False)
                op.waits = waits
                op.clock = dict(clk)
                self.eops[e].append(op)

    def emit(self, final_dma_keys=()):
        nc = self.nc
        for e in self.ENGS:
            cnt = 0
            for op in self.cstream[e]:
                if op.signal:
                    cnt += 1
                    op.sigval = cnt
        import contextlib
        with contextlib.ExitStack() as es:
            sems = {}
            for e in self.ENGS:
                if any(op.signal for op in self.cstream[e]):
                    sems[e] = es.enter_context(nc.semaphore("s_" + e))
            for k in self.dstream:
                sems["d:" + k] = es.enter_context(nc.semaphore("sd_" + k))
            self.nsems = len(sems)
            block = es.enter_context(nc.Block())

            def run(engname):
                def body(eng):
                    for op in self.eops[engname]:
                        for (s, p) in op.waits:
                            if s.startswith("d:"):
                                v = 16 * p
                            else:
                                v = self.cstream[s][p - 1].sigval
                                assert v > 0
                            eng.wait_ge(sems[s], v)
                        if op.fn is None:
                            continue
                        ins = op.fn(eng)
                        if op.dma:
                            ins.then_inc(sems[op.stream], 16)
                        elif op.signal:
                            ins.then_inc(sems[op.stream], 1)
                    if engname == "sp":
                        for k in final_dma_keys:
                            lst = self.dstream.get(k)
                            if lst:
                                eng.wait_ge(sems["d:" + k], 16 * len(lst))
                return body

            block.tensor(run("pe"))
            block.scalar(run("act"))
            block.vector(run("dve"))
            block.gpsimd(run("pool"))
            block.sync(run("sp"))


D = 1024
SEQ = 2048
NMETA = 16
L = SEQ + NMETA
NT = 17
GT = ((0, 1, 2, 3), (4, 5, 6, 7), (8, 9, 10), (11, 12, 13), (14, 15, 16))
DFF = 2816
NJ = 22
LN_EPS = 1e-5
RMS_EPS = 1e-6
ALPHA = 2.0 ** 0.25
INTOT = 4136
O_QLAT, O_KVLAT, O_KROPE, O_FQ, O_FK, O_FV, O_FLOG, O_GATE = 0, 384, 512, 544, 1056, 1568, 2080, 2088


def trows(t):
    return 128 if t < 16 else 16


def gcols(g):
    c0 = GT[g][0] * 128
    n = sum(trows(t) for t in GT[g])
    return c0, n


def build_program():
    nc = bass.Bass("TRN2", target_bir_lowering=False)
    dt_in = lambda n, s: nc.dram_tensor(n, s, F32, kind="ExternalInput").ap()
    x = dt_in("x", [SEQ, D])
    meta = dt_in("meta_tokens", [NMETA, D])
    ln_emb_g = dt_in("ln_emb_g", [D]); ln_emb_b = dt_in("ln_emb_b", [D])
    w_in = dt_in("w_in", [D, INTOT])
    b_gate = dt_in("b_gate", [2 * D]); b_forget = dt_in("b_forget", [8])
    q_norm_g = dt_in("q_norm_g", [384]); w_q_up = dt_in("w_q_up", [384, 768])
    kv_norm_g = dt_in("kv_norm_g", [128]); w_kv_up = dt_in("w_kv_up", [128, 1024])
    w_bm = dt_in("w_branch_mla", [512, D]); w_bf = dt_in("w_branch_fox", [512, D])
    w_out = dt_in("w_out", [D, D])
    ln_mix_g = dt_in("ln_mix_g", [D]); ln_mix_b = dt_in("ln_mix_b", [D])
    w_up = dt_in("w_ffn_up", [D, 2 * DFF])
    conv_w = dt_in("conv_w", [3, DFF]); conv_b = dt_in("conv_b", [DFF])
    w_dn = dt_in("w_ffn_down", [DFF, D])
    ln_ffn_g = dt_in("ln_ffn_g", [D]); ln_ffn_b = dt_in("ln_ffn_b", [D])
    c_ident = dt_in("c_ident", [128, 128]); c_mask = dt_in("c_mask", [128, 128])
    c_cos = dt_in("c_cos", [32, L]); c_sin = dt_in("c_sin", [32, L])
    out = nc.dram_tensor("out", [SEQ, D], F32, kind="ExternalOutput").ap()
    scr_h = nc.dram_tensor("scr_h", [NT * 128, D], F32, kind="Internal").ap()
    scr_h2 = nc.dram_tensor("scr_h2", [NT * 128, D], F32, kind="Internal").ap()
    scr_wu = nc.dram_tensor("scr_wu", [D, 2 * DFF], BF16, kind="Internal").ap()
    scr_wg = nc.dram_tensor("scr_wg", [D, 2 * D], BF16, kind="Internal").ap()
    scr_wb = nc.dram_tensor("scr_wb", [D, D], BF16, kind="Internal").ap()
    scr_wo = nc.dram_tensor("scr_wo", [D, D], BF16, kind="Internal").ap()

    import contextlib
    with contextlib.ExitStack() as es:
        sbt = lambda n, s, d: es.enter_context(nc.sbuf_tensor(n, s, d))
        LP = 2112
        hT = sbt("hT", [128, 8, LP], BF16)
        A1 = sbt("A1", [128, 36864], BF16)
        A2 = sbt("A2", [128, 22528], BF16)
        MISC = sbt("MISC", [128, 10320], BF16)
        LNB = sbt("LNB", [128, 4, 1024], F32)
        GB = sbt("GB", [128, 2, 1024], F32)
        H16 = sbt("H16", [128, 2, 1024], BF16)
        SM = sbt("SM", [128, 256], F32)
        PP = sbt("PP", [128, 128], F32)
        CST = sbt("CST", [128, 2, 128], BF16)
        WK = sbt("WK", [128, 1536], BF16)
        CSTF = A2[:, 4096:4608].bitcast(F32).rearrange("p (a b) -> p a b", a=2)
        IDF = A2[:, 4608:4864].bitcast(F32)
        PPS = A2[:, 4864:5120].bitcast(F32)
        banks = [es.enter_context(nc.psum_tensor("PS%d" % i, [128, 512], F32)) for i in range(8)]
        S = Sched(nc)

        def bv(arena, off, n, dt=BF16):
            if dt == BF16:
                return arena[:, off:off + n]
            return arena[:, off:off + 2 * n].bitcast(F32)

        identb = CST[:, 0, :]
        maskb = CST[:, 1, :]
        PSB = [b[:, :].bitcast(BF16) for b in banks]

        cosT = bv(MISC, 0, L, F32)
        sinT = bv(MISC, 2 * L, L, F32)
        kpeT = bv(MISC, 4 * L, L)
        efb = bv(MISC, 0, L, F32)
        cparts = bv(MISC, 2 * L, 3 * L).rearrange("p (c t) -> p c t", c=3)
        BG = lambda c: PP[:, c:c + 1]
        CW = lambda k, j: PP[:, 16 + k * 22 + j:16 + k * 22 + j + 1]
        CB = lambda j: PP[:, 82 + j:83 + j]
        NH = sbt("NH", [128, 2], F32)
        BF8 = sbt("BF8", [128, 2], F32)
        NEGH = NH[:, 0:2]
        NEGBF = BF8[0:8, 1:2]
        S.op("dve", "memset", NH[:, :], -0.5)

        def consts1():
            S.dma("sp", "c0", CSTF[:, 0, :], c_ident)
            S.dma("sp", "c1", CSTF[:, 1, :], c_mask)
            S.op("dve", "tensor_copy", out=CST[:], in_=CSTF[:])
            S.dma("sp", "c9", GQ[:, 0:384], q_norm_g.partition_broadcast(128))
            S.dma("sp", "c10", GQ[:, 384:512], kv_norm_g.partition_broadcast(128))

        def consts2():
            S.dma("sp", "c3", cosT[64:96, :], c_cos)
            S.dma("sp", "c4", sinT[64:96, :], c_sin)
            S.dma("sp", "c8", BF8[0:8, 0:1], b_forget.unsqueeze(1))
            S.op("dve", "tensor_scalar", out=BF8[0:8, 1:2], in0=BF8[0:8, 0:1], scalar1=-1.0, scalar2=None, op0=ALU.mult)

        def consts3():
            S.dma("sp", "c2", IDF[:], c_ident)
            S.dma("sp", "c5", PPS[0:16, :], b_gate.rearrange("(c p) -> c p", p=128))
            S.dma("sp", "c6", PPS[16:82, :], conv_w.rearrange("k (j p) -> (k j) p", p=128))
            S.dma("sp", "c7", PPS[82:104, :], conv_b.rearrange("(j p) -> j p", p=128))
            S.op("pe", "transpose", out=banks[7][:, 0:104], in_=PPS[0:104, :], identity=IDF[0:104, 0:104])
            S.op("act", "activation", out=PP[:, 0:104], in_=banks[7][:, 0:104], func=AF.Copy)

        smi = [0]

        def sm(n):
            o = (smi[0] % 8) * 32
            smi[0] += 1
            return SM[:, o:o + n]

        def load_ln_params(g_ap, b_ap, tag):
            S.dma("sp", "lng", GB[:, 0, :], g_ap.partition_broadcast(128))
            S.dma("sp", "lnb", GB[:, 1, :], b_ap.partition_broadcast(128))

        h16i = [0]
        SM2 = sbt("SM2", [128, 128], F32)

        def ln_s1(buf, rows):
            s = sm(16)
            S.op("dve", "bn_stats", out=s[0:rows, 0:6], in_=buf[0:rows, 0:512])
            S.op("dve", "bn_stats", out=s[0:rows, 6:12], in_=buf[0:rows, 512:1024])
            S.op("dve", "bn_aggr", out=s[0:rows, 12:14], in_=s[0:rows, 0:12])
            S.op("dve", "tensor_scalar", out=s[0:rows, 14:15], in0=s[0:rows, 13:14], scalar1=LN_EPS, scalar2=None, op0=ALU.add)
            S.op("pool", "tensor_tensor", out=s[0:rows, 15:16], in0=s[0:rows, 14:15], in1=NEGH[0:rows, 0:1], op=ALU.pow)
            return s

        def ln_s2(buf, rows, s):
            S.op("dve", "tensor_scalar", out=s[0:rows, 14:15], in0=s[0:rows, 12:13], scalar1=s[0:rows, 15:16], scalar2=-1.0, op0=ALU.mult, op1=ALU.mult)
            S.op("act", "activation", out=buf[0:rows, :], in_=buf[0:rows, :], func=AF.Identity, scale=s[0:rows, 15:16], bias=s[0:rows, 14:15])

        def ln_s3(buf, rows):
            S.op("pool", "tensor_tensor", out=buf[0:rows, :], in0=buf[0:rows, :], in1=GB[0:rows, 0, :], op=ALU.mult)

        def ln_s4(buf, rows, want16=True, eng="dve"):
            S.op(eng, "tensor_tensor", out=buf[0:rows, :], in0=buf[0:rows, :], in1=GB[0:rows, 1, :], op=ALU.add)
            if want16:
                h16i[0] += 1
                S.op("act", "activation", out=H16[0:rows, h16i[0] % 2, :], in_=buf[0:rows, :], func=AF.Copy)
                return h16i[0] % 2

        def layernorm(buf, rows, out32=None, want16=True):
            s = ln_s1(buf, rows)
            ln_s2(buf, rows, s)
            ln_s3(buf, rows)
            return ln_s4(buf, rows, want16)

        tri = [0]

        def transpose_to_hT(t, rows, hi):
            pb = PSB[6 + (tri[0] % 2)]
            tri[0] += 1
            for c in range(8):
                S.op("pe", "transpose", out=pb[:, c * 128:c * 128 + rows], in_=H16[0:rows, hi, c * 128:(c + 1) * 128], identity=identb[0:rows, 0:rows])
            S.op("act", "activation", out=hT[:, :, t * 128:t * 128 + rows],
                 in_=pb[:, :].rearrange("p (c t) -> p c t", c=8)[:, :, 0:rows], func=AF.Copy)

        def wload(key, dst3, src2, col0, ncols):
            c = 0
            i = 0
            while c < ncols:
                w = min(1024, ncols - c)
                S.dma("pool", "%s_%d" % (key, i % 2), dst3[:, :, c:c + w],
                      src2[:, col0 + c:col0 + c + w].rearrange("(kc p) n -> p kc n", p=128))
                c += w
                i += 1

        latT = bv(A1, 0, 4 * LP).rearrange("p (c t) -> p c t", c=4)
        QK = [bv(A1, 8448 + i * LP, LP) for i in range(4)]
        VA = [bv(A1, 16896 + i * 2176, 2176).rearrange("p (t c) -> p t c", c=128) for i in range(2)]
        wfox = bv(A1, 21248, 8 * 1536).rearrange("p (k n) -> p k n", k=8)
        wlat = bv(A1, 21248, 8 * 512).rearrange("p (k n) -> p k n", k=8)
        wkr = bv(A1, 25344, 8 * 192).rearrange("p (k n) -> p k n", k=8)
        wfl = bv(A1, 26880, 8 * 8).rearrange("p (k n) -> p k n", k=8)
        latn = [bv(A1, 26944 + i * 512, 512) for i in range(2)]
        GQ = bv(A1, 27968, 512, F32)
        omT = bv(A2, 0, 4 * L).rearrange("p (c t) -> p c t", c=4)
        ofT = bv(A2, 8256, 4 * L).rearrange("p (c t) -> p c t", c=4)
        wq = bv(A2, 16512, 3 * 768).rearrange("p (k n) -> p k n", k=3)
        wqB = bv(A2, 18816, 3 * 768).rearrange("p (k h d) -> p k h d", k=3, h=8)
        wkv = bv(A2, 21120, 1024)
        PT = [bv(WK, i * 512, 512) for i in range(3)]
        rt1 = LNB[:, 0, 0:512]
        rt2 = LNB[:, 1, 0:512]
        sbs = [LNB[:, 2, 0:512], LNB[:, 3, 0:512]]
        RT = bv(A2, 0, 1024, F32).rearrange("p (a b) -> p a b", a=2)
        junk = bv(A2, 2048, 512)
        junk2 = bv(A2, 2560, 512)

        load_ln_params(ln_emb_g, ln_emb_b, "emb")
        wload("wl", wlat, w_in, 0, 512)
        def prep1():
            S.op("pool", "memset", wkr[:, :, :], 0.0)
            S.dma("pool", "wk0", wkr[:, :, 64:96], w_in[:, O_KROPE:O_KROPE + 32].rearrange("(kc p) n -> p kc n", p=128))
            S.dma("pool", "wk1", wkr[:, :, 160:176], w_in[:, O_KROPE + 16:O_KROPE + 32].rearrange("(kc p) n -> p kc n", p=128))
            S.dma("pool", "wk2", wkr[:, :, 176:192], w_in[:, O_KROPE:O_KROPE + 16].rearrange("(kc p) n -> p kc n", p=128))
            S.dma("pool", "wk3", wfl[:, :, :], w_in[:, O_FLOG:O_FLOG + 8].rearrange("(kc p) n -> p kc n", p=128))

        def prep2():
            S.dma("pool", "wq", wq[:, :, :], w_q_up.rearrange("(kc p) n -> p kc n", p=128))
            S.dma("pool", "wkv", wkv[:, :], w_kv_up)

        def prep3():
            S.op("pool", "memset", wqB[:, :, :, :], 0.0)
            wq4 = wq.rearrange("p k (h d) -> p k h d", h=8)
            S.op("pool", "tensor_copy", out=wqB[:, :, :, 64:80], in_=wq4[:, :, :, 80:96])
            S.op("pool", "tensor_copy", out=wqB[:, :, :, 80:96], in_=wq4[:, :, :, 64:80])

        hooksAB = {1: consts1, 2: prep1, 3: consts2, 5: prep2, 8: prep3, 12: consts3}

        def rope(PA, PB, dst, c0, n, t1=None, t2=None):
            t1 = rt1 if t1 is None else t1
            t2 = rt2 if t2 is None else t2
            S.op("dve", "tensor_tensor", out=t1[64:96, 0:n], in0=PA[64:96, 0:n], in1=cosT[64:96, c0:c0 + n], op=ALU.mult)
            S.op("dve", "tensor_tensor", out=t2[64:96, 0:n], in0=PB[64:96, 0:n], in1=sinT[64:96, c0:c0 + n], op=ALU.mult)
            S.op("dve", "tensor_tensor", out=dst[64:96, c0:c0 + n], in0=t1[64:96, 0:n], in1=t2[64:96, 0:n], op=ALU.add)

        st = {}

        XS = [LNB[:, i, :] for i in range(4)] + [bv(A2, 6144 + i * 2048, 1024, F32) for i in range(2)]

        def sA0(t):
            rows = trows(t)
            xb = XS[t % 6]
            if t == 0:
                S.dma("sp", "xm", xb[0:16, :], meta)
                S.dma("sp", "x0", xb[16:128, :], x[0:112, :])
            else:
                S.dma("sp", "x%d" % (t % 6), xb[0:rows, :], x[t * 128 - 16:t * 128 - 16 + rows, :])

        def sA1(t):
            st[t] = ln_s1(XS[t % 6], trows(t))

        def sA2(t):
            ln_s2(XS[t % 6], trows(t), st[t])

        def sA3(t):
            ln_s3(XS[t % 6], trows(t))

        def sA4(t):
            rows = trows(t)
            hb = XS[t % 6]
            st[t] = ln_s4(hb, rows)
            S.dma("sp", "hs%d" % (t % 6), scr_h[t * 128:t * 128 + rows, :], hb[0:rows, :])

        def sA5(t):
            rows = trows(t)
            transpose_to_hT(t, rows, st[t])
            for g in range(5):
                if GT[g][-1] == t:
                    c0, n = gcols(g)
                    PA, PBk = banks[4], banks[5]
                    for kc in range(8):
                        S.op("pe", "matmul", out=PA[0:96, 0:n], lhsT=wkr[:, kc, 0:96], rhs=hT[:, kc, c0:c0 + n], start=(kc == 0), stop=(kc == 7))
                    for kc in range(8):
                        S.op("pe", "matmul", out=PBk[0:96, 0:n], lhsT=wkr[:, kc, 96:192], rhs=hT[:, kc, c0:c0 + n], start=(kc == 0), stop=(kc == 7))
                    rope(PA, PBk, kpeT, c0, n, RT[:, 0, :], RT[:, 1, :])
                    PF = banks[4]
                    for kc in range(8):
                        S.op("pe", "matmul", out=PF[0:8, 0:n], lhsT=wfl[:, kc, 0:8], rhs=hT[:, kc, c0:c0 + n], start=(kc == 0), stop=(kc == 7))
                    S.op("act", "activation", out=efb[0:8, c0:c0 + n], in_=PF[0:8, 0:n], func=AF.Exp, scale=-1.0, bias=NEGBF)
                    S.op("act", "activation", out=efb[0:8, c0:c0 + n], in_=efb[0:8, c0:c0 + n], func=AF.Ln, bias=1.0)

        def sB6(t):
            rows = trows(t)
            PL = banks[t % 3]
            for kc in range(8):
                S.op("pe", "matmul", out=PL[0:rows, :], lhsT=hT[:, kc, t * 128:t * 128 + rows], rhs=wlat[:, kc, :], start=(kc == 0), stop=(kc == 7))
            s_ = SM2[:, (t % 4) * 32:(t % 4) * 32 + 8]
            jk = junk if t % 2 == 0 else junk2
            S.op("act", "activation", out=jk[0:rows, 0:384], in_=PL[0:rows, 0:384], func=AF.Square, accum_out=s_[0:rows, 0:1])
            S.op("act", "activation", out=jk[0:rows, 384:512], in_=PL[0:rows, 384:512], func=AF.Square, accum_out=s_[0:rows, 1:2])

        def sB7(t):
            rows = trows(t)
            PL = banks[t % 3]
            s_ = SM2[:, (t % 4) * 32:(t % 4) * 32 + 8]
            S.op("dve", "tensor_scalar", out=s_[0:rows, 2:3], in0=s_[0:rows, 0:1], scalar1=1.0 / 384, scalar2=RMS_EPS, op0=ALU.mult, op1=ALU.add)
            S.op("dve", "tensor_scalar", out=s_[0:rows, 3:4], in0=s_[0:rows, 1:2], scalar1=1.0 / 128, scalar2=RMS_EPS, op0=ALU.mult, op1=ALU.add)
            S.op("pool", "tensor_tensor", out=s_[0:rows, 4:6], in0=s_[0:rows, 2:4], in1=NEGH[0:rows, 0:2], op=ALU.pow)

        def sB8(t):
            rows = trows(t)
            PL = banks[t % 3]
            s_ = SM2[:, (t % 4) * 32:(t % 4) * 32 + 8]
            ln_ = latn[t % 2]
            S.op("dve", "scalar_tensor_tensor", out=ln_[0:rows, 0:384], in0=PL[0:rows, 0:384], scalar=s_[0:rows, 4:5], in1=GQ[0:rows, 0:384], op0=ALU.mult, op1=ALU.mult)
            S.op("dve", "scalar_tensor_tensor", out=ln_[0:rows, 384:512], in0=PL[0:rows, 384:512], scalar=s_[0:rows, 5:6], in1=GQ[0:rows, 384:512], op0=ALU.mult, op1=ALU.mult)

        def sB9(t):
            rows = trows(t)
            ln_ = latn[t % 2]
            pb = PSB[3]
            for c in range(4):
                S.op("pe", "transpose", out=pb[:, c * 128:c * 128 + rows], in_=ln_[0:rows, c * 128:(c + 1) * 128], identity=identb[0:rows, 0:rows])
            S.op("act", "activation", out=latT[:, :, t * 128:t * 128 + rows],
                 in_=pb[:, 0:512].rearrange("p (c t) -> p c t", c=4)[:, :, 0:rows], func=AF.Copy)

        def run_skewed(stages, items):
            n_ = len(items)
            for k in range(n_ + len(stages) - 1):
                for si in range(len(stages) - 1, -1, -1):
                    i_ = k - si
                    if 0 <= i_ < n_:
                        stages[si](items[i_])

        stagesAB = [sA0, sA1, sA2, sA3, sA4, sA5, sB6, sB7, sB8, sB9]
        for k in range(NT + len(stagesAB) - 1):
            if k in hooksAB:
                hooksAB[k]()
            for si in (9, 8, 7, 5, 6, 4, 3, 2, 1, 0):
                t = k - si
                if 0 <= t < NT:
                    stagesAB[si](t)

        def scan_chunk():
            onesb = bv(A2, 8256, 1024, F32)
            r1b = bv(A2, 8256 + 2048, L, F32)
            S.op("dve", "memset", onesb[0:8, :], 1.0)
            prev = None
            for a in range(0, L, 1024):
                w = min(1024, L - a)
                S.op("dve", "tensor_tensor_scan", out=efb[0:8, a:a + w], data0=onesb[0:8, 0:w], data1=efb[0:8, a:a + w],
                     initial=(0.0 if prev is None else prev), op0=ALU.mult, op1=ALU.add)
                prev = efb[0:8, a + w - 1:a + w]
            S.op("dve", "tensor_copy", out=cparts[0:8, 0, :], in_=efb[0:8, :])
            S.op("dve", "tensor_tensor", out=r1b[0:8, :], in0=efb[0:8, :], in1=cparts[0:8, 0, :], op=ALU.subtract)
            S.op("dve", "tensor_copy", out=cparts[0:8, 1, :], in_=r1b[0:8, :])
            S.op("dve", "tensor_tensor", out=r1b[0:8, :], in0=r1b[0:8, :], in1=cparts[0:8, 1, :], op=ALU.subtract)
            S.op("dve", "tensor_copy", out=cparts[0:8, 2, :], in_=r1b[0:8, :])


        def wfox_chunk():
            wfqk = wfox[:, :, 0:1024].rearrange("p k (h a c) -> p k h a c", h=8, a=2)
            for kc in range(8):
                S.dma("pool", "wfq%d" % kc, wfqk[:, kc, :, 0, :], w_in[kc * 128:(kc + 1) * 128, O_FQ:O_FQ + 512].rearrange("p (h c) -> p h c", h=8))
                S.dma("pool", "wfk%d" % kc, wfqk[:, kc, :, 1, :], w_in[kc * 128:(kc + 1) * 128, O_FK:O_FK + 512].rearrange("p (h c) -> p h c", h=8))
            S.dma("pool", "wf_2", wfox[:, :, 1024:1536], w_in[:, O_FV:O_FV + 512].rearrange("(kc p) n -> p kc n", p=128))

        ring = [0]

        def nbank():
            b = banks[ring[0] % 4]
            ring[0] += 1
            return b

        PTR = PT + [LNB[:, k, 512:1024].bitcast(BF16)[:, i * 512:(i + 1) * 512] for k in range(3) for i in range(2)]
        KT2 = PTR[7:9]
        PTR = PTR[0:7]
        kt2i = [0]
        VH = bv(A1, 0, NT * 256).rearrange("p (t c) -> p t c", c=256)
        ptc = [0]
        otc = [0]
        SKEW = 4

        def attention(h, b, krows, scale, oT, chunks):
            qT, kT, va = QK[b], QK[2 + b], VA[b]
            steps = []
            gpar = {}
            for g in range(5):
                c0, n = gcols(g)
                gpar[g] = otc[0] % 2
                otc[0] += 1
                tl = [t for t in range(NT) if t <= GT[g][-1]]
                for j in tl:
                    steps.append((g, j, j == tl[0], j == tl[-1]))

            def qk(step):
                g, j, first, last = step
                c0, n = gcols(g)
                kr = trows(j)
                r = j - GT[g][0]
                off = 128 * r if r > 0 else 0
                diag = r >= 0
                w = n - off
                ST = nbank()
                ptb = PTR[ptc[0] % len(PTR)]
                ptc[0] += 1
                S.op("pe", "matmul", out=ST[0:kr, 0:w], lhsT=kT[0:krows, j * 128:j * 128 + kr], rhs=qT[0:krows, c0 + off:c0 + n], start=True, stop=not diag)
                if diag:
                    dw = min(128, w)
                    S.op("pe", "matmul", out=ST[0:kr, 0:dw], lhsT=identb[0:kr, 0:kr], rhs=maskb[0:kr, 0:dw], start=False, stop=True)
                S.op("act", "activation", out=ptb[0:kr, 0:w], in_=ST[0:kr, 0:w], func=AF.Exp, scale=scale)
                return (step, ptb, kr, off, w)

            def pv(info):
                (g, j, first, last), ptb, kr, off, w = info
                c0, n = gcols(g)
                OT = banks[6 + gpar[g]]
                S.op("pe", "matmul", out=OT[:, off:n], lhsT=va[0:kr, j, :], rhs=ptb[0:kr, 0:w], start=first, stop=last)
                if last:
                    sb_ = sbs[gpar[g]]
                    if h % 2 == 0:
                        o_lo, s_lo = 0, 64
                    else:
                        o_lo, s_lo = 64, 0
                    S.op("act", "activation", out=sb_[o_lo:o_lo + 64, 0:n], in_=OT[s_lo:s_lo + 64, 0:n], func=AF.Ln)
                    S.op("act", "activation", out=sb_[o_lo:o_lo + 64, 0:n], in_=sb_[o_lo:o_lo + 64, 0:n], func=AF.Exp, scale=-1.0)
                    S.op("dve", "tensor_tensor", out=oT[o_lo:o_lo + 64, h // 2, c0:c0 + n], in0=OT[o_lo:o_lo + 64, 0:n], in1=sb_[o_lo:o_lo + 64, 0:n], op=ALU.mult)

            infos = []
            for idx, st in enumerate(steps):
                infos.append(qk(st))
                if idx >= SKEW:
                    pv(infos[idx - SKEW])
                if idx % 5 == 4 and chunks:
                    chunks.pop(0)()
            for idx in range(max(0, len(steps) - SKEW), len(steps)):
                pv(infos[idx])
            while chunks:
                chunks.pop(0)()

        def v_evac(PV, tiles, va, h):
            off = 0 if h % 2 == 0 else 64
            full = [t for t in tiles if trows(t) == 128]
            if full:
                S.op("dve", "tensor_copy", out=va[:, full[0]:full[0] + len(full), off:off + 64],
                     in_=PV[:, 0:64 * len(full)].rearrange("p (t c) -> p t c", c=64))
            if len(full) < len(tiles):
                i = len(full)
                S.op("dve", "tensor_copy", out=va[0:16, tiles[i], off:off + 64], in_=PV[0:16, 64 * i:64 * (i + 1)])

        def mla_chunks(h, b):
            qT, kT, va = QK[b], QK[2 + b], VA[b]

            def qpart(g):
                c0, n = gcols(g)
                PA, PBq = banks[4], banks[5]
                for kc in range(3):
                    S.op("pe", "matmul", out=PA[0:96, 0:n], lhsT=wq[:, kc, h * 96:(h + 1) * 96], rhs=latT[:, kc, c0:c0 + n], start=(kc == 0), stop=(kc == 2))
                for kc in range(3):
                    S.op("pe", "matmul", out=PBq[0:96, 0:n], lhsT=wqB[:, kc, h, :], rhs=latT[:, kc, c0:c0 + n], start=(kc == 0), stop=(kc == 2))
                S.op("dve", "tensor_copy", out=qT[0:64, c0:c0 + n], in_=PA[0:64, 0:n])
                rope(PA, PBq, qT, c0, n)

            def kvpart(g):
                c0, n = gcols(g)
                PK, PV = banks[4], banks[5]
                S.op("pe", "matmul", out=PK[0:64, 0:n], lhsT=wkv[:, h * 128:h * 128 + 64], rhs=latT[:, 3, c0:c0 + n], start=True, stop=True)
                S.op("dve", "tensor_copy", out=kT[0:64, c0:c0 + n], in_=PK[0:64, 0:n])
                S.op("pool", "tensor_copy", out=kT[64:96, c0:c0 + n], in_=kpeT[64:96, c0:c0 + n])
                for i, t in enumerate(GT[g]):
                    rows = trows(t)
                    S.op("pe", "matmul", out=PV[0:rows, 64 * i:64 * (i + 1)], lhsT=latT[:, 3, t * 128:t * 128 + rows], rhs=wkv[:, h * 128 + 64:h * 128 + 128], start=True, stop=True)
                v_evac(PV, GT[g], va, h)
            parts = []
            for g in range(5):
                parts.append(lambda g=g: qpart(g))
                parts.append(lambda g=g: kvpart(g))
            return parts

        augk = [0]

        def fox_chunks(h, b):
            qT, kT, va = QK[b], QK[2 + b], VA[b]

            def qkpart(g):
                c0, n = gcols(g)
                if g == 0:
                    S.op("pool", "memset", qT[64:70, :], 1.0)
                    S.op("pool", "memset", kT[64:70, :], -1.0)
                    for i in range(3):
                        S.dma("sp", "aug%d" % (augk[0] % 6), qT[64 + i:65 + i, 0:L], cparts[h:h + 1, i, :]); augk[0] += 1
                        S.dma("sp", "aug%d" % (augk[0] % 6), kT[67 + i:68 + i, 0:L], cparts[h:h + 1, i, :]); augk[0] += 1
                PA = banks[4]
                for kc in range(8):
                    S.op("pe", "matmul", out=PA[:, 0:n], lhsT=wfox[:, kc, h * 128:(h + 1) * 128], rhs=hT[:, kc, c0:c0 + n], start=(kc == 0), stop=(kc == 7))
                S.op("dve", "tensor_scalar", out=qT[0:64, c0:c0 + n], in0=PA[0:64, 0:n], scalar1=0.125, scalar2=None, op0=ALU.mult)
                kt = KT2[kt2i[0] % 2]
                kt2i[0] += 1
                S.op("dve", "tensor_copy", out=kt[64:128, 0:n], in_=PA[64:128, 0:n])
                S.dma("sp", "ksh%d" % (kt2i[0] % 4), kT[0:64, c0:c0 + n], kt[64:128, 0:n])

            def vpart(g):
                if h % 4 == 0:
                    tl = list(GT[g])
                    for a in range(0, len(tl), 2):
                        PV = banks[5]
                        pair = tl[a:a + 2]
                        for i, t in enumerate(pair):
                            rows = trows(t)
                            for kc in range(8):
                                S.op("pe", "matmul", out=PV[0:rows, 256 * i:256 * (i + 1)], lhsT=hT[:, kc, t * 128:t * 128 + rows],
                                     rhs=wfox[:, kc, 1024 + h * 64:1024 + h * 64 + 256], start=(kc == 0), stop=(kc == 7))
                        full = [t for t in pair if trows(t) == 128]
                        if full:
                            S.op("dve", "tensor_copy", out=VH[:, full[0]:full[0] + len(full), :],
                                 in_=PV[:, 0:256 * len(full)].rearrange("p (t c) -> p t c", c=256))
                        if len(full) < len(pair):
                            i = len(full)
                            S.op("dve", "tensor_copy", out=VH[0:16, pair[i], :], in_=PV[0:16, 256 * i:256 * (i + 1)])
                off = 0 if h % 2 == 0 else 64
                hq = (h % 4) * 64
                tl = list(GT[g])
                full = [t for t in tl if trows(t) == 128]
                S.op("pool", "tensor_copy", out=va[:, full[0]:full[0] + len(full), off:off + 64], in_=VH[:, full[0]:full[0] + len(full), hq:hq + 64])
                if len(full) < len(tl):
                    S.op("pool", "tensor_copy", out=va[0:16, tl[-1], off:off + 64], in_=VH[0:16, tl[-1], hq:hq + 64])
            parts = []
            for g in range(5):
                parts.append(lambda g=g: qkpart(g))
                parts.append(lambda g=g: vpart(g))
            return parts

        for b in range(2):
            S.op("pool", "memset", VA[b][:, :, :], 1.0)

        pc_blocks = []
        for rb in range(8):
            for c_ in (0, 1024):
                pc_blocks.append((scr_wg[rb * 128:(rb + 1) * 128, c_:c_ + 1024], w_in[rb * 128:(rb + 1) * 128, O_GATE + c_:O_GATE + c_ + 1024]))
        for rb in range(4):
            pc_blocks.append((scr_wb[rb * 128:(rb + 1) * 128, :], w_bm[rb * 128:(rb + 1) * 128, :]))
        for rb in range(4):
            pc_blocks.append((scr_wb[512 + rb * 128:512 + (rb + 1) * 128, :], w_bf[rb * 128:(rb + 1) * 128, :]))
        for rb in range(8):
            pc_blocks.append((scr_wo[rb * 128:(rb + 1) * 128, :], w_out[rb * 128:(rb + 1) * 128, :]))
        for rb in range(8):
            for c_ in range(0, 2 * DFF, 1024):
                w_ = min(1024, 2 * DFF - c_)
                pc_blocks.append((scr_wu[rb * 128:(rb + 1) * 128, c_:c_ + w_], w_up[rb * 128:(rb + 1) * 128, c_:c_ + w_]))
        PC_PER = 7
        pcn = [0]

        def precast_chunk(ci):
            for (dst_, src_) in pc_blocks[ci * PC_PER:(ci + 1) * PC_PER]:
                S.dma("pool", "pc%d" % (pcn[0] % 8), dst_, src_)
                pcn[0] += 1

        heads = [("m", h) for h in range(8)] + [("f", h) for h in range(8)]

        def chunks_for(i):
            kind, h = heads[i]
            return mla_chunks(h, i % 2) if kind == "m" else fox_chunks(h, i % 2)

        for c in chunks_for(0):
            c()
        for i, (kind, h) in enumerate(heads):
            nxt = chunks_for(i + 1) if i + 1 < len(heads) else []
            if i == 1:
                nxt.append(scan_chunk)
            if i == 0:
                nxt.insert(2, wfox_chunk)
            if 2 <= i <= 13:
                nxt.insert(5, (lambda i=i: precast_chunk(i - 2)))
            if kind == "m":
                attention(h, i % 2, 96, 96.0 ** -0.5, omT, nxt)
            else:
                attention(h, i % 2, 70, 1.0, ofT, nxt)

        wgate = bv(A1, 0, 8 * 2048).rearrange("p (k n) -> p k n", k=8)
        wbm = bv(A1, 16384, 4 * 1024).rearrange("p (k n) -> p k n", k=4)
        wbf = bv(A1, 20480, 4 * 1024).rearrange("p (k n) -> p k n", k=4)
        wout = bv(A1, 24576, 8 * 1024).rearrange("p (k n) -> p k n", k=8)
        mergedT = bv(A1, 32768, 8 * 512).rearrange("p (c t) -> p c t", c=8)
        EB = [[bv(MISC, (i * 4 + k) * 1024, 512, F32) for k in range(4)] for i in range(2)]
        for pi, c_ in enumerate((0, 1024, 512, 1536)):
            S.dma("pool", "wg_%d" % pi, wgate[:, :, c_:c_ + 512], scr_wg[:, c_:c_ + 512].rearrange("(kc p) n -> p kc n", p=128))
        S.dma("pool", "wbm_0", wbm[:, :, :], scr_wb[0:512, :].rearrange("(kc p) n -> p kc n", p=128))
        S.dma("pool", "wbf_0", wbf[:, :, :], scr_wb[512:1024, :].rearrange("(kc p) n -> p kc n", p=128))
        S.dma("pool", "wo_0", wout[:, :, :], scr_wo.rearrange("(kc p) n -> p kc n", p=128))
        load_ln_params(ln_mix_g, ln_mix_b, "mix")
        deferred = []

        def c_iter(g, c):
            c0, n = gcols(g)
            sg1, sg2, m1, m2 = EB[c % 2]
            PG1, PG2, PB1, PB2 = banks[0], banks[1], banks[2], banks[3]
            for kc in range(8):
                S.op("pe", "matmul", out=PG1[:, 0:n], lhsT=wgate[:, kc, c * 128:(c + 1) * 128], rhs=hT[:, kc, c0:c0 + n], start=(kc == 0), stop=(kc == 7))
            S.op("act", "activation", out=sg1[:, 0:n], in_=PG1[:, 0:n], func=AF.Sigmoid, bias=BG(c))
            for kc in range(8):
                S.op("pe", "matmul", out=PG2[:, 0:n], lhsT=wgate[:, kc, 1024 + c * 128:1024 + (c + 1) * 128], rhs=hT[:, kc, c0:c0 + n], start=(kc == 0), stop=(kc == 7))
            S.op("act", "activation", out=sg2[:, 0:n], in_=PG2[:, 0:n], func=AF.Sigmoid, bias=BG(8 + c))
            for kc in range(4):
                S.op("pe", "matmul", out=PB1[:, 0:n], lhsT=wbm[:, kc, c * 128:(c + 1) * 128], rhs=omT[:, kc, c0:c0 + n], start=(kc == 0), stop=(kc == 3))
            S.op("dve", "tensor_tensor", out=m1[:, 0:n], in0=PB1[:, 0:n], in1=sg1[:, 0:n], op=ALU.mult)
            for kc in range(4):
                S.op("pe", "matmul", out=PB2[:, 0:n], lhsT=wbf[:, kc, c * 128:(c + 1) * 128], rhs=ofT[:, kc, c0:c0 + n], start=(kc == 0), stop=(kc == 3))
            S.op("dve", "tensor_tensor", out=m2[:, 0:n], in0=PB2[:, 0:n], in1=sg2[:, 0:n], op=ALU.mult)
            S.op("pool", "tensor_tensor", out=mergedT[:, c, 0:n], in0=m1[:, 0:n], in1=m2[:, 0:n], op=ALU.add)

        c_iters = [[(lambda g=g, c=c: c_iter(g, c)) for c in range(8)] for g in range(5)]
        stE = {}

        def e1(it):
            ti, t = it
            rows = trows(t)
            hb = LNB[:, t % 4, :]
            for cg in range(2):
                PM = banks[4 + cg]
                for c in range(8):
                    S.op("pe", "matmul", out=PM[0:rows, :], lhsT=mergedT[:, c, ti * 128:ti * 128 + rows], rhs=wout[:, c, cg * 512:(cg + 1) * 512], start=(c == 0), stop=(c == 7))
                S.op("dve", "scalar_tensor_tensor", out=hb[0:rows, cg * 512:(cg + 1) * 512], in0=hb[0:rows, cg * 512:(cg + 1) * 512], scalar=ALPHA, in1=PM[0:rows, :], op0=ALU.mult, op1=ALU.add)

        def e2(it):
            stE[it[1]] = ln_s1(LNB[:, it[1] % 4, :], trows(it[1]))

        def e3(it):
            ln_s2(LNB[:, it[1] % 4, :], trows(it[1]), stE[it[1]])

        def e4(it):
            ln_s3(LNB[:, it[1] % 4, :], trows(it[1]))

        def e5(it):
            t = it[1]
            rows = trows(t)
            tb = LNB[:, t % 4, :]
            while len(deferred) > 1:
                deferred.pop(0)()
            hi = ln_s4(tb, rows)
            S.dma("sp", "hs%d" % (t % 4), scr_h2[t * 128:t * 128 + rows, :], tb[0:rows, :])
            deferred.append(lambda t=t, rows=rows, hi=hi: transpose_to_hT(t, rows, hi))

        for g in range(5):
            for t in GT[g]:
                S.dma("sp", "hr%d" % (t % 4), LNB[0:trows(t), t % 4, :], scr_h[t * 128:t * 128 + trows(t), :])
            first = True
            while c_iters[g]:
                c_iters[g].pop(0)()
                if first:
                    first = False
                    while deferred:
                        deferred.pop(0)()
            items = list(enumerate(GT[g]))
            for it in items:
                e1(it)
            stages = [e2, e3, e4, e5]
            n_ = len(items)
            for k in range(n_ + len(stages) - 1):
                for si in range(len(stages) - 1, -1, -1):
                    i_ = k - si
                    if 0 <= i_ < n_:
                        stages[si](items[i_])
                if g + 1 < 5 and len(c_iters[g + 1]) > 1:
                    c_iters[g + 1].pop(0)()


        WU = [bv(A1, i * 4096, 8 * 512).rearrange("p (k n) -> p k n", k=8) for i in range(4)] + [bv(MISC, i * 4096, 8 * 512).rearrange("p (k n) -> p k n", k=8) for i in range(2)]
        actT = bv(A1, 16384, NJ * 512).rearrange("p (j t) -> p j t", j=NJ)
        GO = 30
        Gb = [bv(A1, 27648 + i * 1152, 548, F32) for i in range(2)]
        cvb = [bv(A1, 29952 + i * 1024, 512, F32) for i in range(2)]
        slb = [bv(A1, 32000 + i * 1024, 512, F32) for i in range(2)]
        carry = bv(A1, 34048, NJ * 2, F32).rearrange("p (j c) -> p j c", c=2)
        wdn = bv(A2, 0, NJ * 1024).rearrange("p (j n) -> p j n", j=NJ)
        S.op("dve", "memset", carry[:, :, :], 0.0)
        load_ln_params(ln_ffn_g, ln_ffn_b, "ffn")
        rounds = [(g, j) for g in range(5) for j in range(0, NJ, 4)]

        def issue_round(ri):
            g_, j_ = rounds[ri]
            nj = min(4, NJ - j_)
            rs = (ri % 3) * 2
            S.dma("pool", "wug%d" % (ri % 3), WU[rs][:, :, 0:nj * 128], scr_wu[:, j_ * 128:(j_ + nj) * 128].rearrange("(kc p) n -> p kc n", p=128))
            S.dma("pool", "wuv%d" % (ri % 3), WU[rs + 1][:, :, 0:nj * 128], scr_wu[:, DFF + j_ * 128:DFF + (j_ + nj) * 128].rearrange("(kc p) n -> p kc n", p=128))

        issue_round(0)
        issue_round(1)
        ri = -1
        for g in range(5):
            c0, n = gcols(g)
            for t in GT[g]:
                S.dma("sp", "hr%d" % (t % 4), LNB[0:trows(t), t % 4, :], scr_h2[t * 128:t * 128 + trows(t), :])
            pend_mult = None
            pend_silu = None
            for j in range(NJ):
                jj = j % 4
                if jj == 0:
                    ri += 1
                    if ri + 2 < len(rounds):
                        issue_round(ri + 2)
                    wg_, wv_ = WU[(ri % 3) * 2], WU[(ri % 3) * 2 + 1]
                    if g == 0 and j == 4:
                        while deferred:
                            deferred.pop(0)()
                        S.dma("pool", "wd0", wdn[:, :, :], w_dn.rearrange("(j p) n -> p j n", p=128))
                pj = j % 2
                PUg, PUv = banks[(j % 3) * 2], banks[(j % 3) * 2 + 1]
                for kc in range(8):
                    S.op("pe", "matmul", out=PUg[:, 0:n], lhsT=wg_[:, kc, jj * 128:(jj + 1) * 128], rhs=hT[:, kc, c0:c0 + n], start=(kc == 0), stop=(kc == 7))
                for kc in range(8):
                    S.op("pe", "matmul", out=PUv[:, 0:n], lhsT=wv_[:, kc, jj * 128:(jj + 1) * 128], rhs=hT[:, kc, c0:c0 + n], start=(kc == 0), stop=(kc == 7))
                G_ = Gb[pj]
                cv = cvb[pj]
                sl = slb[pj]
                S.op("act", "activation", out=G_[:, GO + 2:GO + 2 + n], in_=PUg[:, 0:n], func=AF.Copy)
                S.op("act", "activation", out=G_[:, GO:GO + 2], in_=carry[:, j, :], func=AF.Copy)
                if pend_silu is not None:
                    pend_silu()
                S.op("dve", "tensor_scalar", out=cv[:, 0:n], in0=G_[:, GO + 2:GO + 2 + n], scalar1=CW(2, j), scalar2=CB(j), op0=ALU.mult, op1=ALU.add)
                S.op("dve", "scalar_tensor_tensor", out=cv[:, 0:n], in0=G_[:, GO + 1:GO + 1 + n], scalar=CW(1, j), in1=cv[:, 0:n], op0=ALU.mult, op1=ALU.add)
                S.op("dve", "scalar_tensor_tensor", out=cv[:, 0:n], in0=G_[:, GO:GO + n], scalar=CW(0, j), in1=cv[:, 0:n], op0=ALU.mult, op1=ALU.add)
                S.op("act", "activation", out=carry[:, j, :], in_=G_[:, GO + n:GO + n + 2], func=AF.Copy)
                if pend_mult is not None:
                    pend_mult()
                pend_silu = (lambda cv=cv, sl=sl: S.op("act", "activation", out=sl[:, 0:n], in_=cv[:, 0:n], func=AF.Silu))
                pend_mult = (lambda j=j, PUv=PUv, sl=sl: S.op("dve", "tensor_tensor", out=actT[:, j, 0:n], in0=PUv[:, 0:n], in1=sl[:, 0:n], op=ALU.mult))
            pend_silu()
            pend_mult()
            stF = {}

            def f1(it):
                ti, t = it
                rows = trows(t)
                hb = LNB[:, t % 4, :]
                for cg in range(2):
                    PD = banks[4 + (ti % 2) * 2 + cg]
                    for j in range(NJ):
                        S.op("pe", "matmul", out=PD[0:rows, :], lhsT=actT[:, j, ti * 128:ti * 128 + rows], rhs=wdn[:, j, cg * 512:(cg + 1) * 512], start=(j == 0), stop=(j == NJ - 1))
                    S.op("dve", "scalar_tensor_tensor", out=hb[0:rows, cg * 512:(cg + 1) * 512], in0=hb[0:rows, cg * 512:(cg + 1) * 512], scalar=ALPHA, in1=PD[0:rows, :], op0=ALU.mult, op1=ALU.add)

            def f2(it):
                stF[it[1]] = ln_s1(LNB[:, it[1] % 4, :], trows(it[1]))

            def f3(it):
                ln_s2(LNB[:, it[1] % 4, :], trows(it[1]), stF[it[1]])

            def f4(it):
                ln_s3(LNB[:, it[1] % 4, :], trows(it[1]))

            def f5(it):
                t = it[1]
                rows = trows(t)
                ob = LNB[:, t % 4, :]
                ln_s4(ob, rows, want16=False, eng="pool")
                if t == 0:
                    S.dma("sp", "o%d" % (t % 4), out[0:112, :], ob[16:128, :])
                else:
                    S.dma("sp", "o%d" % (t % 4), out[t * 128 - 16:t * 128 - 16 + rows, :], ob[0:rows, :])

            run_skewed([f1, f2, f3, f4, f5], list(enumerate(GT[g])))

        S.emit(final_dma_keys=["o0", "o1", "o2", "o3"])
    return nc


_CACHE = {}


def _consts():
    ident = np.eye(128, dtype=np.float32)
    k = np.arange(128)[:, None]
    q = np.arange(128)[None, :]
    mask = np.where(q >= k, 0.0, -30000.0).astype(np.float32)
    half = 16
    inv_freq = (np.float32(10000.0) ** (-np.arange(half, dtype=np.float32) / np.float32(half))).astype(np.float32)
    pos = np.arange(L, dtype=np.float32)
    ang = (pos[None, :] * inv_freq[:, None]).astype(np.float32)
    cos = np.cos(ang).astype(np.float32)
    sin = np.sin(ang).astype(np.float32)
    c_cos = np.concatenate([cos, cos], axis=0)
    c_sin = np.concatenate([-sin, sin], axis=0)
    return ident, mask, np.ascontiguousarray(c_cos), np.ascontiguousarray(c_sin)


def kernel(**inputs):
    if "nc" not in _CACHE:
        _CACHE["nc"] = build_program()
    nc = _CACHE["nc"]
    f = lambda a: np.ascontiguousarray(np.asarray(a, dtype=np.float32))
    ident, mask, c_cos, c_sin = _consts()
    shared = {
        "meta_tokens": f(inputs["meta_tokens"]),
        "ln_emb_g": f(inputs["ln_emb_g"]), "ln_emb_b": f(inputs["ln_emb_b"]),
        "w_in": f(inputs["w_in"])[0], "b_gate": f(inputs["b_gate"])[0], "b_forget": f(inputs["b_forget"])[0],
        "q_norm_g": f(inputs["q_norm_g"])[0], "w_q_up": f(inputs["w_q_up"])[0],
        "kv_norm_g": f(inputs["kv_norm_g"])[0], "w_kv_up": f(inputs["w_kv_up"])[0],
        "w_branch_mla": f(inputs["w_branch_mla"])[0], "w_branch_fox": f(inputs["w_branch_fox"])[0],
        "w_out": f(inputs["w_out"])[0],
        "ln_mix_g": f(inputs["ln_mix_g"])[0], "ln_mix_b": f(inputs["ln_mix_b"])[0],
        "w_ffn_up": f(inputs["w_ffn_up"])[0], "conv_w": f(inputs["conv_w"])[0], "conv_b": f(inputs["conv_b"])[0],
        "w_ffn_down": f(inputs["w_ffn_down"])[0],
        "ln_ffn_g": f(inputs["ln_ffn_g"])[0], "ln_ffn_b": f(inputs["ln_ffn_b"])[0],
        "c_ident": ident, "c_mask": mask, "c_cos": c_cos, "c_sin": c_sin,
    }
    x = f(inputs["x"])
    in_maps = []
    for b in range(8):
        m = dict(shared)
        m["x"] = x[b]
        in_maps.append(m)
    res = run_bass_kernel_spmd(nc, in_maps, core_ids=list(range(8)))
    return np.stack([np.asarray(r["out"], dtype=np.float32) for r in res.results], axis=0)
```

```python
import numpy as np
import concourse.bass as bass
import concourse.mybir as mybir
from concourse.bass_utils import run_bass_kernel_spmd

F32 = mybir.dt.float32
BF16 = mybir.dt.bfloat16
AF = mybir.ActivationFunctionType
ALU = mybir.AluOpType
AX = mybir.AxisListType


class _Op:
    __slots__ = ("eng", "stream", "pos", "fn", "waits", "clock", "signal", "sigval", "dma")

    def __init__(self, eng, stream, pos, fn, dma):
        self.eng = eng
        self.stream = stream
        self.pos = pos
        self.fn = fn
        self.waits = []
        self.clock = None
        self.signal = dma
        self.sigval = 0
        self.dma = dma


class Sched:
    ENGS = ("pe", "act", "dve", "pool", "sp")
    G = 128

    def __init__(self, nc):
        self.nc = nc
        self.eops = {e: [] for e in self.ENGS}
        self.cstream = {e: [] for e in self.ENGS}
        self.dstream = {}
        self.eclock = {e: {} for e in self.ENGS}
        self.state = {}
        self._gcache = {}

    def _keys(self, ap):
        t = ap.tensor
        name = t.name
        dims = [tuple(d) for d in ap.ap]
        ck = (name, ap.offset, tuple(dims), str(ap.dtype))
        r = self._gcache.get(ck)
        if r is not None:
            return r
        space = str(ap.space)
        if "PSUM" in space.upper():
            pstride, pcount = dims[0]
            p0 = ap.offset // pstride if pstride > 0 else 0
            r = tuple(("PSUM", name, q) for q in range(p0 // 32, (p0 + pcount - 1) // 32 + 1))
            self._gcache[ck] = r
            return r
        n = ap.size()
        esz = ap.nbytes() // max(1, n) if n else 4
        if esz == 0:
            esz = 4
        isdram = "DRAM" in space.upper() or "HBM" in space.upper()
        if isdram:
            fdims = dims
            foff = ap.offset
            qs = (0,)
            G = 4096
        else:
            pstride, pcount = dims[0]
            fdims = dims[1:]
            p0 = ap.offset // pstride if pstride > 0 else 0
            foff = ap.offset - p0 * pstride
            qs = tuple(range(p0 // 32, (p0 + pcount - 1) // 32 + 1))
            G = self.G
        fdims = [d for d in fdims if d[1] > 1 and d[0] != 0]
        if not fdims:
            ranges = [(foff, foff + 1)]
        else:
            fd = sorted(fdims, key=lambda d: abs(d[0]))
            s0, n0 = fd[0]
            outer = fd[1:]
            cnt = 1
            for d in outer:
                cnt *= d[1]
            if cnt > 256:
                lo = foff
                hi = foff + sum((d[1] - 1) * d[0] for d in fd) + 1
                ranges = [(lo, hi)]
            else:
                starts = [foff]
                for (s, c) in outer:
                    starts = [b + i * s for b in starts for i in range(c)]
                ln = (n0 - 1) * s0 + 1
                ranges = [(b, b + ln) for b in starts]
        gs = set()
        for lo, hi in ranges:
            for g in range((lo * esz) // G, (hi * esz - 1) // G + 1):
                gs.add(g)
        r = tuple((name, q, g) for q in qs for g in gs)
        self._gcache[ck] = r
        return r

    def _add(self, eng, fn, reads, writes, dma_key=None):
        dma = dma_key is not None
        if dma:
            lst = self.dstream.setdefault(dma_key, [])
            stream = "d:" + dma_key
            op = _Op(eng, stream, len(lst) + 1, fn, True)
        else:
            lst = self.cstream[eng]
            stream = eng
            op = _Op(eng, stream, len(lst) + 1, fn, False)
        deps = set()
        if dma and lst:
            deps.add(lst[-1])
        rk = []
        wk = []
        pk = []
        for ap in reads:
            for k in self._keys(ap):
                (pk if k[0] == "PSUM" else rk).append(k)
        for ap in writes:
            wk.extend(self._keys(ap))
        st = self.state
        for k in rk:
            s = st.get(k)
            if s is not None and s[0] is not None:
                deps.add(s[0])
        for k in pk:
            s = st.get(k)
            if s is not None:
                if s[0] is not None:
                    deps.add(s[0])
                for rs_, ro_ in s[1].items():
                    if rs_ != stream:
                        deps.add(ro_)
        for k in wk:
            s = st.get(k)
            if s is not None:
                if s[0] is not None:
                    deps.add(s[0])
                deps.update(s[1].values())
        clk = self.eclock[eng]
        for d in sorted(deps, key=lambda d: -d.pos):
            if d.stream == "pe" and eng == "pe" and not dma:
                continue
            if clk.get(d.stream, 0) >= d.pos:
                continue
            op.waits.append((d.stream, d.pos))
            d.signal = True
            for s, p in d.clock.items():
                if clk.get(s, 0) < p:
                    clk[s] = p
        c = dict(clk)
        c[stream] = op.pos
        op.clock = c
        for k in rk + pk:
            s = st.get(k)
            if s is None:
                st[k] = [None, {stream: op}]
            else:
                s[1][stream] = op
        for k in wk:
            st[k] = [op, {}]
        lst.append(op)
        self.eops[eng].append(op)
        return op

    def pe(self, fn, r=(), w=()):
        return self._add("pe", fn, r, w)

    def act(self, fn, r=(), w=()):
        return self._add("act", fn, r, w)

    def dve(self, fn, r=(), w=()):
        return self._add("dve", fn, r, w)

    def pool(self, fn, r=(), w=()):
        return self._add("pool", fn, r, w)

    def op(self, eng, name, *args, **kw):
        wr = [kw[k] for k in ("out", "accum_out") if isinstance(kw.get(k), bass.AP)]
        rd = [v for k, v in kw.items() if k not in ("out", "accum_out") and isinstance(v, bass.AP)]
        if name == "memset":
            wr = [args[0]]
        return self._add(eng, lambda e: getattr(e, name)(*args, **kw), rd, wr)

    def dma(self, eng, key, out, in_, **kw):
        return self._add(eng, lambda e: e.dma_start(out=out, in_=in_, **kw), [in_], [out], dma_key=key)

    def barrier_all(self, aps_by_eng=None):
        last = {}
        for e in self.ENGS:
            if self.cstream[e]:
                last[e] = self.cstream[e][-1]
        for k, lst in self.dstream.items():
            if lst:
                last["d:" + k] = lst[-1]
        for e in ("pe", "act", "dve", "pool", "sp"):
            clk = self.eclock[e]
            waits = []
            for s, d in last.items():
                if clk.get(s, 0) >= d.pos:
                    continue
                waits.append((s, d.pos))
                d.signal = True
                for s2, p in d.clock.items():
                    if clk.get(s2, 0) < p:
                        clk[s2] = p
            if waits:
                op = _Op(e, "nop", 0, None, False)
                op.waits = waits
                op.clock = dict(clk)
                self.eops[e].append(op)

    def emit(self, final_dma_keys=()):
        nc = self.nc
        for e in self.ENGS:
            cnt = 0
            for op in self.cstream[e]:
                if op.signal:
                    cnt += 1
                    op.sigval = cnt
        import contextlib
        with contextlib.ExitStack() as es:
            sems = {}
            for e in self.ENGS:
                if any(op.signal for op in self.cstream[e]):
                    sems[e] = es.enter_context(nc.semaphore("s_" + e))
            for k in self.dstream:
                sems["d:" + k] = es.enter_context(nc.semaphore("sd_" + k))
            self.nsems = len(sems)
            block = es.enter_context(nc.Block())

            def run(engname):
                def body(eng):
                    for op in self.eops[engname]:
                        for (s, p) in op.waits:
                            if s.startswith("d:"):
                                v = 16 * p
                            else:
                                v = self.cstream[s][p - 1].sigval
                                assert v > 0
                            eng.wait_ge(sems[s], v)
                        if op.fn is None:
                            continue
                        ins = op.fn(eng)
                        if op.dma:
                            ins.then_inc(sems[op.stream], 16)
                        elif op.signal:
                            ins.then_inc(sems[op.stream], 1)
                    if engname == "sp":
                        for k in final_dma_keys:
                            lst = self.dstream.get(k)
                            if lst:
                                eng.wait_ge(sems["d:" + k], 16 * len(lst))
                return body

            block.tensor(run("pe"))
            block.scalar(run("act"))
            block.vector(run("dve"))
            block.gpsimd(run("pool"))
            block.sync(run("sp"))


D = 1024
SEQ = 2048
NMETA = 16
L = SEQ + NMETA
NT = 17
GT = ((0, 1, 2, 3), (4, 5, 6, 7), (8, 9, 10), (11, 12, 13), (14, 15, 16))
DFF = 2816
NJ = 22
LN_EPS = 1e-5
RMS_EPS = 1e-6
ALPHA = 2.0 ** 0.25
INTOT = 4136
O_QLAT, O_KVLAT, O_KROPE, O_FQ, O_FK, O_FV, O_FLOG, O_GATE = 0, 384, 512, 544, 1056, 1568, 2080, 2088


def trows(t):
    return 128 if t < 16 else 16


def gcols(g):
    c0 = GT[g][0] * 128
    n = sum(trows(t) for t in GT[g])
    return c0, n


def build_program():
    nc = bass.Bass("TRN2", target_bir_lowering=False)
    dt_in = lambda n, s: nc.dram_tensor(n, s, F32, kind="ExternalInput").ap()
    x = dt_in("x", [SEQ, D])
    meta = dt_in("meta_tokens", [NMETA, D])
    ln_emb_g = dt_in("ln_emb_g", [D]); ln_emb_b = dt_in("ln_emb_b", [D])
    w_in = dt_in("w_in", [D, INTOT])
    b_gate = dt_in("b_gate", [2 * D]); b_forget = dt_in("b_forget", [8])
    q_norm_g = dt_in("q_norm_g", [384]); w_q_up = dt_in("w_q_up", [384, 768])
    kv_norm_g = dt_in("kv_norm_g", [128]); w_kv_up = dt_in("w_kv_up", [128, 1024])
    w_bm = dt_in("w_branch_mla", [512, D]); w_bf = dt_in("w_branch_fox", [512, D])
    w_out = dt_in("w_out", [D, D])
    ln_mix_g = dt_in("ln_mix_g", [D]); ln_mix_b = dt_in("ln_mix_b", [D])
    w_up = dt_in("w_ffn_up", [D, 2 * DFF])
    conv_w = dt_in("conv_w", [3, DFF]); conv_b = dt_in("conv_b", [DFF])
    w_dn = dt_in("w_ffn_down", [DFF, D])
    ln_ffn_g = dt_in("ln_ffn_g", [D]); ln_ffn_b = dt_in("ln_ffn_b", [D])
    c_ident = dt_in("c_ident", [128, 128]); c_mask = dt_in("c_mask", [128, 128])
    c_cos = dt_in("c_cos", [32, L]); c_sin = dt_in("c_sin", [32, L])
    out = nc.dram_tensor("out", [SEQ, D], F32, kind="ExternalOutput").ap()
    scr_h = nc.dram_tensor("scr_h", [NT * 128, D], F32, kind="Internal").ap()
    scr_h2 = nc.dram_tensor("scr_h2", [NT * 128, D], F32, kind="Internal").ap()
    scr_wu = nc.dram_tensor("scr_wu", [D, 2 * DFF], BF16, kind="Internal").ap()
    scr_wg = nc.dram_tensor("scr_wg", [D, 2 * D], BF16, kind="Internal").ap()
    scr_wb = nc.dram_tensor("scr_wb", [D, D], BF16, kind="Internal").ap()
    scr_wo = nc.dram_tensor("scr_wo", [D, D], BF16, kind="Internal").ap()

    import contextlib
    with contextlib.ExitStack() as es:
        sbt = lambda n, s, d: es.enter_context(nc.sbuf_tensor(n, s, d))
        LP = 2112
        hT = sbt("hT", [128, 8, LP], BF16)
        A1 = sbt("A1", [128, 36864], BF16)
        A2 = sbt("A2", [128, 22528], BF16)
        MISC = sbt("MISC", [128, 10320], BF16)
        LNB = sbt("LNB", [128, 4, 1024], F32)
        GB = sbt("GB", [128, 2, 1024], F32)
        H16 = sbt("H16", [128, 3, 1024], BF16)
        SM = sbt("SM", [128, 256], F32)
        PP = sbt("PP", [128, 128], F32)
        CST = sbt("CST", [128, 2, 128], BF16)
        WK = sbt("WK", [128, 1536], BF16)
        CSTF = A2[:, 4096:4608].bitcast(F32).rearrange("p (a b) -> p a b", a=2)
        IDF = A2[:, 4608:4864].bitcast(F32)
        PPS = A2[:, 4864:5120].bitcast(F32)
        banks = [es.enter_context(nc.psum_tensor("PS%d" % i, [128, 512], F32)) for i in range(8)]
        S = Sched(nc)

        def bv(arena, off, n, dt=BF16):
            if dt == BF16:
                return arena[:, off:off + n]
            return arena[:, off:off + 2 * n].bitcast(F32)

        identb = CST[:, 0, :]
        maskb = CST[:, 1, :]
        PSB = [b[:, :].bitcast(BF16) for b in banks]

        cosT = bv(MISC, 0, L, F32)
        sinT = bv(MISC, 2 * L, L, F32)
        kpeT = bv(MISC, 4 * L, L)
        efb = bv(MISC, 0, L, F32)
        cparts = bv(MISC, 2 * L, 3 * L).rearrange("p (c t) -> p c t", c=3)
        BG = lambda c: PP[:, c:c + 1]
        CW = lambda k, j: PP[:, 16 + k * 22 + j:16 + k * 22 + j + 1]
        CB = lambda j: PP[:, 82 + j:83 + j]
        NH = sbt("NH", [128, 2], F32)
        BF8 = sbt("BF8", [128, 2], F32)
        NEGH = NH[:, 0:2]
        NEGBF = BF8[0:8, 1:2]
        S.op("dve", "memset", NH[:, :], -0.5)

        def consts1():
            S.dma("sp", "c0", CSTF[:, 0, :], c_ident)
            S.dma("sp", "c1", CSTF[:, 1, :], c_mask)
            S.op("dve", "tensor_copy", out=CST[:], in_=CSTF[:])
            S.dma("sp", "c9", GQ[:, 0:384], q_norm_g.partition_broadcast(128))
            S.dma("sp", "c10", GQ[:, 384:512], kv_norm_g.partition_broadcast(128))

        def consts2():
            S.dma("sp", "c3", cosT[64:96, :], c_cos)
            S.dma("sp", "c4", sinT[64:96, :], c_sin)
            S.dma("sp", "c8", BF8[0:8, 0:1], b_forget.unsqueeze(1))
            S.op("dve", "tensor_scalar", out=BF8[0:8, 1:2], in0=BF8[0:8, 0:1], scalar1=-1.0, scalar2=None, op0=ALU.mult)

        def consts3():
            S.dma("sp", "c2", IDF[:], c_ident)
            S.dma("sp", "c5", PPS[0:16, :], b_gate.rearrange("(c p) -> c p", p=128))
            S.dma("sp", "c6", PPS[16:82, :], conv_w.rearrange("k (j p) -> (k j) p", p=128))
            S.dma("sp", "c7", PPS[82:104, :], conv_b.rearrange("(j p) -> j p", p=128))
            S.op("pe", "transpose", out=banks[7][:, 0:104], in_=PPS[0:104, :], identity=IDF[0:104, 0:104])
            S.op("act", "activation", out=PP[:, 0:104], in_=banks[7][:, 0:104], func=AF.Copy)

        smi = [0]

        def sm(n):
            o = (smi[0] % 8) * 32
            smi[0] += 1
            return SM[:, o:o + n]

        def load_ln_params(g_ap, b_ap, tag):
            S.dma("sp", "lng", GB[:, 0, :], g_ap.partition_broadcast(128))
            S.dma("sp", "lnb", GB[:, 1, :], b_ap.partition_broadcast(128))

        h16i = [0]
        SM2 = sbt("SM2", [128, 128], F32)

        def ln_s1(buf, rows):
            s = sm(16)
            S.op("dve", "bn_stats", out=s[0:rows, 0:6], in_=buf[0:rows, 0:512])
            S.op("dve", "bn_stats", out=s[0:rows, 6:12], in_=buf[0:rows, 512:1024])
            S.op("dve", "bn_aggr", out=s[0:rows, 12:14], in_=s[0:rows, 0:12])
            S.op("dve", "tensor_scalar", out=s[0:rows, 14:15], in0=s[0:rows, 13:14], scalar1=LN_EPS, scalar2=None, op0=ALU.add)
            S.op("pool", "tensor_tensor", out=s[0:rows, 15:16], in0=s[0:rows, 14:15], in1=NEGH[0:rows, 0:1], op=ALU.pow)
            return s

        def ln_s2(buf, rows, s):
            S.op("dve", "tensor_scalar", out=s[0:rows, 14:15], in0=s[0:rows, 12:13], scalar1=s[0:rows, 15:16], scalar2=-1.0, op0=ALU.mult, op1=ALU.mult)
            S.op("act", "activation", out=buf[0:rows, :], in_=buf[0:rows, :], func=AF.Identity, scale=s[0:rows, 15:16], bias=s[0:rows, 14:15])

        def ln_s3(buf, rows):
            S.op("pool", "tensor_tensor", out=buf[0:rows, :], in0=buf[0:rows, :], in1=GB[0:rows, 0, :], op=ALU.mult)

        def ln_s4(buf, rows, want16=True):
            S.op("dve", "tensor_tensor", out=buf[0:rows, :], in0=buf[0:rows, :], in1=GB[0:rows, 1, :], op=ALU.add)
            if want16:
                h16i[0] += 1
                S.op("act", "activation", out=H16[0:rows, h16i[0] % 3, :], in_=buf[0:rows, :], func=AF.Copy)
                return h16i[0] % 3

        def layernorm(buf, rows, out32=None, want16=True):
            s = ln_s1(buf, rows)
            ln_s2(buf, rows, s)
            ln_s3(buf, rows)
            return ln_s4(buf, rows, want16)

        tri = [0]

        def transpose_to_hT(t, rows, hi):
            pb = PSB[6 + (tri[0] % 2)]
            tri[0] += 1
            for c in range(8):
                S.op("pe", "transpose", out=pb[:, c * 128:c * 128 + rows], in_=H16[0:rows, hi, c * 128:(c + 1) * 128], identity=identb[0:rows, 0:rows])
            S.op("act", "activation", out=hT[:, :, t * 128:t * 128 + rows],
                 in_=pb[:, :].rearrange("p (c t) -> p c t", c=8)[:, :, 0:rows], func=AF.Copy)

        def wload(key, dst3, src2, col0, ncols):
            c = 0
            i = 0
            while c < ncols:
                w = min(1024, ncols - c)
                S.dma("pool", "%s_%d" % (key, i % 2), dst3[:, :, c:c + w],
                      src2[:, col0 + c:col0 + c + w].rearrange("(kc p) n -> p kc n", p=128))
                c += w
                i += 1

        latT = bv(A1, 0, 4 * LP).rearrange("p (c t) -> p c t", c=4)
        QK = [bv(A1, 8448 + i * LP, LP) for i in range(4)]
        VA = [bv(A1, 16896 + i * 2176, 2176).rearrange("p (t c) -> p t c", c=128) for i in range(2)]
        wfox = bv(A1, 21248, 8 * 1536).rearrange("p (k n) -> p k n", k=8)
        wlat = bv(A1, 21248, 8 * 512).rearrange("p (k n) -> p k n", k=8)
        wkr = bv(A1, 25344, 8 * 192).rearrange("p (k n) -> p k n", k=8)
        wfl = bv(A1, 26880, 8 * 8).rearrange("p (k n) -> p k n", k=8)
        latn = [bv(A1, 26944 + i * 512, 512) for i in range(2)]
        GQ = bv(A1, 27968, 512, F32)
        omT = bv(A2, 0, 4 * L).rearrange("p (c t) -> p c t", c=4)
        ofT = bv(A2, 8256, 4 * L).rearrange("p (c t) -> p c t", c=4)
        wq = bv(A2, 16512, 3 * 768).rearrange("p (k n) -> p k n", k=3)
        wqB = bv(A2, 18816, 3 * 768).rearrange("p (k h d) -> p k h d", k=3, h=8)
        wkv = bv(A2, 21120, 1024)
        PT = [bv(WK, i * 512, 512) for i in range(3)]
        rt1 = LNB[:, 0, 0:512]
        rt2 = LNB[:, 1, 0:512]
        sbs = [LNB[:, 2, 0:512], LNB[:, 3, 0:512]]
        RT = bv(A2, 0, 1024, F32).rearrange("p (a b) -> p a b", a=2)
        junk = bv(A2, 2048, 512)
        junk2 = bv(A2, 2560, 512)

        load_ln_params(ln_emb_g, ln_emb_b, "emb")
        wload("wl", wlat, w_in, 0, 512)
        def prep1():
            S.op("pool", "memset", wkr[:, :, :], 0.0)
            S.dma("pool", "wk0", wkr[:, :, 64:96], w_in[:, O_KROPE:O_KROPE + 32].rearrange("(kc p) n -> p kc n", p=128))
            S.dma("pool", "wk1", wkr[:, :, 160:176], w_in[:, O_KROPE + 16:O_KROPE + 32].rearrange("(kc p) n -> p kc n", p=128))
            S.dma("pool", "wk2", wkr[:, :, 176:192], w_in[:, O_KROPE:O_KROPE + 16].rearrange("(kc p) n -> p kc n", p=128))
            S.dma("pool", "wk3", wfl[:, :, :], w_in[:, O_FLOG:O_FLOG + 8].rearrange("(kc p) n -> p kc n", p=128))

        def prep2():
            S.dma("pool", "wq", wq[:, :, :], w_q_up.rearrange("(kc p) n -> p kc n", p=128))
            S.dma("pool", "wkv", wkv[:, :], w_kv_up)

        def prep3():
            S.op("pool", "memset", wqB[:, :, :, :], 0.0)
            wq4 = wq.rearrange("p k (h d) -> p k h d", h=8)
            S.op("pool", "tensor_copy", out=wqB[:, :, :, 64:80], in_=wq4[:, :, :, 80:96])
            S.op("pool", "tensor_copy", out=wqB[:, :, :, 80:96], in_=wq4[:, :, :, 64:80])

        hooksAB = {1: consts1, 2: prep1, 3: consts2, 5: prep2, 8: prep3, 12: consts3}

        def rope(PA, PB, dst, c0, n, t1=None, t2=None):
            t1 = rt1 if t1 is None else t1
            t2 = rt2 if t2 is None else t2
            S.op("dve", "tensor_tensor", out=t1[64:96, 0:n], in0=PA[64:96, 0:n], in1=cosT[64:96, c0:c0 + n], op=ALU.mult)
            S.op("dve", "tensor_tensor", out=t2[64:96, 0:n], in0=PB[64:96, 0:n], in1=sinT[64:96, c0:c0 + n], op=ALU.mult)
            S.op("dve", "tensor_tensor", out=dst[64:96, c0:c0 + n], in0=t1[64:96, 0:n], in1=t2[64:96, 0:n], op=ALU.add)

        st = {}

        XS = [LNB[:, i, :] for i in range(4)] + [bv(A2, 6144 + i * 2048, 1024, F32) for i in range(2)]

        def sA0(t):
            rows = trows(t)
            xb = XS[t % 6]
            if t == 0:
                S.dma("sp", "xm", xb[0:16, :], meta)
                S.dma("sp", "x0", xb[16:128, :], x[0:112, :])
            else:
                S.dma("sp", "x%d" % (t % 6), xb[0:rows, :], x[t * 128 - 16:t * 128 - 16 + rows, :])

        def sA1(t):
            st[t] = ln_s1(XS[t % 6], trows(t))

        def sA2(t):
            ln_s2(XS[t % 6], trows(t), st[t])

        def sA3(t):
            ln_s3(XS[t % 6], trows(t))

        def sA4(t):
            rows = trows(t)
            hb = XS[t % 6]
            st[t] = ln_s4(hb, rows)
            S.dma("sp", "hs%d" % (t % 6), scr_h[t * 128:t * 128 + rows, :], hb[0:rows, :])

        def sA5(t):
            rows = trows(t)
            transpose_to_hT(t, rows, st[t])
            for g in range(5):
                if GT[g][-1] == t:
                    c0, n = gcols(g)
                    PA, PBk = banks[4], banks[5]
                    for kc in range(8):
                        S.op("pe", "matmul", out=PA[0:96, 0:n], lhsT=wkr[:, kc, 0:96], rhs=hT[:, kc, c0:c0 + n], start=(kc == 0), stop=(kc == 7))
                    for kc in range(8):
                        S.op("pe", "matmul", out=PBk[0:96, 0:n], lhsT=wkr[:, kc, 96:192], rhs=hT[:, kc, c0:c0 + n], start=(kc == 0), stop=(kc == 7))
                    rope(PA, PBk, kpeT, c0, n, RT[:, 0, :], RT[:, 1, :])
                    PF = banks[4]
                    for kc in range(8):
                        S.op("pe", "matmul", out=PF[0:8, 0:n], lhsT=wfl[:, kc, 0:8], rhs=hT[:, kc, c0:c0 + n], start=(kc == 0), stop=(kc == 7))
                    S.op("act", "activation", out=efb[0:8, c0:c0 + n], in_=PF[0:8, 0:n], func=AF.Exp, scale=-1.0, bias=NEGBF)
                    S.op("act", "activation", out=efb[0:8, c0:c0 + n], in_=efb[0:8, c0:c0 + n], func=AF.Ln, bias=1.0)

        def sB6(t):
            rows = trows(t)
            PL = banks[t % 3]
            for kc in range(8):
                S.op("pe", "matmul", out=PL[0:rows, :], lhsT=hT[:, kc, t * 128:t * 128 + rows], rhs=wlat[:, kc, :], start=(kc == 0), stop=(kc == 7))
            s_ = SM2[:, (t % 4) * 32:(t % 4) * 32 + 8]
            jk = junk if t % 2 == 0 else junk2
            S.op("act", "activation", out=jk[0:rows, 0:384], in_=PL[0:rows, 0:384], func=AF.Square, accum_out=s_[0:rows, 0:1])
            S.op("act", "activation", out=jk[0:rows, 384:512], in_=PL[0:rows, 384:512], func=AF.Square, accum_out=s_[0:rows, 1:2])

        def sB7(t):
            rows = trows(t)
            PL = banks[t % 3]
            s_ = SM2[:, (t % 4) * 32:(t % 4) * 32 + 8]
            S.op("dve", "tensor_scalar", out=s_[0:rows, 2:3], in0=s_[0:rows, 0:1], scalar1=1.0 / 384, scalar2=RMS_EPS, op0=ALU.mult, op1=ALU.add)
            S.op("dve", "tensor_scalar", out=s_[0:rows, 3:4], in0=s_[0:rows, 1:2], scalar1=1.0 / 128, scalar2=RMS_EPS, op0=ALU.mult, op1=ALU.add)
            S.op("pool", "tensor_tensor", out=s_[0:rows, 4:6], in0=s_[0:rows, 2:4], in1=NEGH[0:rows, 0:2], op=ALU.pow)

        def sB8(t):
            rows = trows(t)
            PL = banks[t % 3]
            s_ = SM2[:, (t % 4) * 32:(t % 4) * 32 + 8]
            ln_ = latn[t % 2]
            S.op("dve", "scalar_tensor_tensor", out=ln_[0:rows, 0:384], in0=PL[0:rows, 0:384], scalar=s_[0:rows, 4:5], in1=GQ[0:rows, 0:384], op0=ALU.mult, op1=ALU.mult)
            S.op("dve", "scalar_tensor_tensor", out=ln_[0:rows, 384:512], in0=PL[0:rows, 384:512], scalar=s_[0:rows, 5:6], in1=GQ[0:rows, 384:512], op0=ALU.mult, op1=ALU.mult)

        def sB9(t):
            rows = trows(t)
            ln_ = latn[t % 2]
            pb = PSB[3]
            for c in range(4):
                S.op("pe", "transpose", out=pb[:, c * 128:c * 128 + rows], in_=ln_[0:rows, c * 128:(c + 1) * 128], identity=identb[0:rows, 0:rows])
            S.op("act", "activation", out=latT[:, :, t * 128:t * 128 + rows],
                 in_=pb[:, 0:512].rearrange("p (c t) -> p c t", c=4)[:, :, 0:rows], func=AF.Copy)

        def run_skewed(stages, items):
            n_ = len(items)
            for k in range(n_ + len(stages) - 1):
                for si in range(len(stages) - 1, -1, -1):
                    i_ = k - si
                    if 0 <= i_ < n_:
                        stages[si](items[i_])

        stagesAB = [sA0, sA1, sA2, sA3, sA4, sA5, sB6, sB7, sB8, sB9]
        for k in range(NT + len(stagesAB) - 1):
            if k in hooksAB:
                hooksAB[k]()
            for si in (9, 8, 7, 5, 6, 4, 3, 2, 1, 0):
                t = k - si
                if 0 <= t < NT:
                    stagesAB[si](t)

        def scan_chunk():
            onesb = bv(A2, 8256, 1024, F32)
            r1b = bv(A2, 8256 + 2048, L, F32)
            S.op("dve", "memset", onesb[0:8, :], 1.0)
            prev = None
            for a in range(0, L, 1024):
                w = min(1024, L - a)
                S.op("dve", "tensor_tensor_scan", out=efb[0:8, a:a + w], data0=onesb[0:8, 0:w], data1=efb[0:8, a:a + w],
                     initial=(0.0 if prev is None else prev), op0=ALU.mult, op1=ALU.add)
                prev = efb[0:8, a + w - 1:a + w]
            S.op("dve", "tensor_copy", out=cparts[0:8, 0, :], in_=efb[0:8, :])
            S.op("dve", "tensor_tensor", out=r1b[0:8, :], in0=efb[0:8, :], in1=cparts[0:8, 0, :], op=ALU.subtract)
            S.op("dve", "tensor_copy", out=cparts[0:8, 1, :], in_=r1b[0:8, :])
            S.op("dve", "tensor_tensor", out=r1b[0:8, :], in0=r1b[0:8, :], in1=cparts[0:8, 1, :], op=ALU.subtract)
            S.op("dve", "tensor_copy", out=cparts[0:8, 2, :], in_=r1b[0:8, :])


        def wfox_chunk():
            wfqk = wfox[:, :, 0:1024].rearrange("p k (h a c) -> p k h a c", h=8, a=2)
            for kc in range(8):
                S.dma("pool", "wfq%d" % kc, wfqk[:, kc, :, 0, :], w_in[kc * 128:(kc + 1) * 128, O_FQ:O_FQ + 512].rearrange("p (h c) -> p h c", h=8))
                S.dma("pool", "wfk%d" % kc, wfqk[:, kc, :, 1, :], w_in[kc * 128:(kc + 1) * 128, O_FK:O_FK + 512].rearrange("p (h c) -> p h c", h=8))
            S.dma("pool", "wf_2", wfox[:, :, 1024:1536], w_in[:, O_FV:O_FV + 512].rearrange("(kc p) n -> p kc n", p=128))

        ring = [0]

        def nbank():
            b = banks[ring[0] % 4]
            ring[0] += 1
            return b

        PTR = PT + [LNB[:, k, 512:1024].bitcast(BF16)[:, i * 512:(i + 1) * 512] for k in range(3) for i in range(2)]
        KT2 = PTR[7:9]
        PTR = PTR[0:7]
        kt2i = [0]
        VH = bv(A1, 0, NT * 256).rearrange("p (t c) -> p t c", c=256)
        ptc = [0]
        otc = [0]
        SKEW = 4

        def attention(h, b, krows, scale, oT, chunks):
            qT, kT, va = QK[b], QK[2 + b], VA[b]
            steps = []
            gpar = {}
            for g in range(5):
                c0, n = gcols(g)
                gpar[g] = otc[0] % 2
                otc[0] += 1
                tl = [t for t in range(NT) if t <= GT[g][-1]]
                for j in tl:
                    steps.append((g, j, j == tl[0], j == tl[-1]))

            def qk(step):
                g, j, first, last = step
                c0, n = gcols(g)
                kr = trows(j)
                r = j - GT[g][0]
                off = 128 * r if r > 0 else 0
                diag = r >= 0
                w = n - off
                ST = nbank()
                ptb = PTR[ptc[0] % len(PTR)]
                ptc[0] += 1
                S.op("pe", "matmul", out=ST[0:kr, 0:w], lhsT=kT[0:krows, j * 128:j * 128 + kr], rhs=qT[0:krows, c0 + off:c0 + n], start=True, stop=not diag)
                if diag:
                    dw = min(128, w)
                    S.op("pe", "matmul", out=ST[0:kr, 0:dw], lhsT=identb[0:kr, 0:kr], rhs=maskb[0:kr, 0:dw], start=False, stop=True)
                S.op("act", "activation", out=ptb[0:kr, 0:w], in_=ST[0:kr, 0:w], func=AF.Exp, scale=scale)
                return (step, ptb, kr, off, w)

            def pv(info):
                (g, j, first, last), ptb, kr, off, w = info
                c0, n = gcols(g)
                OT = banks[6 + gpar[g]]
                S.op("pe", "matmul", out=OT[:, off:n], lhsT=va[0:kr, j, :], rhs=ptb[0:kr, 0:w], start=first, stop=last)
                if last:
                    sb_ = sbs[gpar[g]]
                    if h % 2 == 0:
                        o_lo, s_lo = 0, 64
                    else:
                        o_lo, s_lo = 64, 0
                    S.op("act", "activation", out=sb_[o_lo:o_lo + 64, 0:n], in_=OT[s_lo:s_lo + 64, 0:n], func=AF.Ln)
                    S.op("act", "activation", out=sb_[o_lo:o_lo + 64, 0:n], in_=sb_[o_lo:o_lo + 64, 0:n], func=AF.Exp, scale=-1.0)
                    S.op("dve", "tensor_tensor", out=oT[o_lo:o_lo + 64, h // 2, c0:c0 + n], in0=OT[o_lo:o_lo + 64, 0:n], in1=sb_[o_lo:o_lo + 64, 0:n], op=ALU.mult)

            infos = []
            for idx, st in enumerate(steps):
                infos.append(qk(st))
                if idx >= SKEW:
                    pv(infos[idx - SKEW])
                if idx % 5 == 4 and chunks:
                    chunks.pop(0)()
            for idx in range(max(0, len(steps) - SKEW), len(steps)):
                pv(infos[idx])
            while chunks:
                chunks.pop(0)()

        def v_evac(PV, tiles, va, h):
            off = 0 if h % 2 == 0 else 64
            full = [t for t in tiles if trows(t) == 128]
            if full:
                S.op("dve", "tensor_copy", out=va[:, full[0]:full[0] + len(full), off:off + 64],
                     in_=PV[:, 0:64 * len(full)].rearrange("p (t c) -> p t c", c=64))
            if len(full) < len(tiles):
                i = len(full)
                S.op("dve", "tensor_copy", out=va[0:16, tiles[i], off:off + 64], in_=PV[0:16, 64 * i:64 * (i + 1)])

        def mla_chunks(h, b):
            qT, kT, va = QK[b], QK[2 + b], VA[b]

            def qpart(g):
                c0, n = gcols(g)
                PA, PBq = banks[4], banks[5]
                for kc in range(3):
                    S.op("pe", "matmul", out=PA[0:96, 0:n], lhsT=wq[:, kc, h * 96:(h + 1) * 96], rhs=latT[:, kc, c0:c0 + n], start=(kc == 0), stop=(kc == 2))
                for kc in range(3):
                    S.op("pe", "matmul", out=PBq[0:96, 0:n], lhsT=wqB[:, kc, h, :], rhs=latT[:, kc, c0:c0 + n], start=(kc == 0), stop=(kc == 2))
                S.op("dve", "tensor_copy", out=qT[0:64, c0:c0 + n], in_=PA[0:64, 0:n])
                rope(PA, PBq, qT, c0, n)

            def kvpart(g):
                c0, n = gcols(g)
                PK, PV = banks[4], banks[5]
                S.op("pe", "matmul", out=PK[0:64, 0:n], lhsT=wkv[:, h * 128:h * 128 + 64], rhs=latT[:, 3, c0:c0 + n], start=True, stop=True)
                S.op("dve", "tensor_copy", out=kT[0:64, c0:c0 + n], in_=PK[0:64, 0:n])
                S.op("pool", "tensor_copy", out=kT[64:96, c0:c0 + n], in_=kpeT[64:96, c0:c0 + n])
                for i, t in enumerate(GT[g]):
                    rows = trows(t)
                    S.op("pe", "matmul", out=PV[0:rows, 64 * i:64 * (i + 1)], lhsT=latT[:, 3, t * 128:t * 128 + rows], rhs=wkv[:, h * 128 + 64:h * 128 + 128], start=True, stop=True)
                v_evac(PV, GT[g], va, h)
            parts = []
            for g in range(5):
                parts.append(lambda g=g: qpart(g))
                parts.append(lambda g=g: kvpart(g))
            return parts

        augk = [0]

        def fox_chunks(h, b):
            qT, kT, va = QK[b], QK[2 + b], VA[b]

            def qkpart(g):
                c0, n = gcols(g)
                if g == 0:
                    S.op("pool", "memset", qT[64:70, :], 1.0)
                    S.op("pool", "memset", kT[64:70, :], -1.0)
                    for i in range(3):
                        S.dma("sp", "aug%d" % (augk[0] % 6), qT[64 + i:65 + i, 0:L], cparts[h:h + 1, i, :]); augk[0] += 1
                        S.dma("sp", "aug%d" % (augk[0] % 6), kT[67 + i:68 + i, 0:L], cparts[h:h + 1, i, :]); augk[0] += 1
                PA = banks[4]
                for kc in range(8):
                    S.op("pe", "matmul", out=PA[:, 0:n], lhsT=wfox[:, kc, h * 128:(h + 1) * 128], rhs=hT[:, kc, c0:c0 + n], start=(kc == 0), stop=(kc == 7))
                S.op("dve", "tensor_scalar", out=qT[0:64, c0:c0 + n], in0=PA[0:64, 0:n], scalar1=0.125, scalar2=None, op0=ALU.mult)
                kt = KT2[kt2i[0] % 2]
                kt2i[0] += 1
                S.op("dve", "tensor_copy", out=kt[64:128, 0:n], in_=PA[64:128, 0:n])
                S.dma("sp", "ksh%d" % (kt2i[0] % 4), kT[0:64, c0:c0 + n], kt[64:128, 0:n])

            def vpart(g):
                if h % 4 == 0:
                    tl = list(GT[g])
                    for a in range(0, len(tl), 2):
                        PV = banks[5]
                        pair = tl[a:a + 2]
                        for i, t in enumerate(pair):
                            rows = trows(t)
                            for kc in range(8):
                                S.op("pe", "matmul", out=PV[0:rows, 256 * i:256 * (i + 1)], lhsT=hT[:, kc, t * 128:t * 128 + rows],
                                     rhs=wfox[:, kc, 1024 + h * 64:1024 + h * 64 + 256], start=(kc == 0), stop=(kc == 7))
                        full = [t for t in pair if trows(t) == 128]
                        if full:
                            S.op("dve", "tensor_copy", out=VH[:, full[0]:full[0] + len(full), :],
                                 in_=PV[:, 0:256 * len(full)].rearrange("p (t c) -> p t c", c=256))
                        if len(full) < len(pair):
                            i = len(full)
                            S.op("dve", "tensor_copy", out=VH[0:16, pair[i], :], in_=PV[0:16, 256 * i:256 * (i + 1)])
                off = 0 if h % 2 == 0 else 64
                hq = (h % 4) * 64
                tl = list(GT[g])
                full = [t for t in tl if trows(t) == 128]
                S.op("pool", "tensor_copy", out=va[:, full[0]:full[0] + len(full), off:off + 64], in_=VH[:, full[0]:full[0] + len(full), hq:hq + 64])
                if len(full) < len(tl):
                    S.op("pool", "tensor_copy", out=va[0:16, tl[-1], off:off + 64], in_=VH[0:16, tl[-1], hq:hq + 64])
            parts = []
            for g in range(5):
                parts.append(lambda g=g: qkpart(g))
                parts.append(lambda g=g: vpart(g))
            return parts

        for b in range(2):
            S.op("pool", "memset", VA[b][:, :, :], 1.0)

        pc_blocks = []
        for rb in range(8):
            for c_ in (0, 1024):
                pc_blocks.append((scr_wg[rb * 128:(rb + 1) * 128, c_:c_ + 1024], w_in[rb * 128:(rb + 1) * 128, O_GATE + c_:O_GATE + c_ + 1024]))
        for rb in range(4):
            pc_blocks.append((scr_wb[rb * 128:(rb + 1) * 128, :], w_bm[rb * 128:(rb + 1) * 128, :]))
        for rb in range(4):
            pc_blocks.append((scr_wb[512 + rb * 128:512 + (rb + 1) * 128, :], w_bf[rb * 128:(rb + 1) * 128, :]))
        for rb in range(8):
            pc_blocks.append((scr_wo[rb * 128:(rb + 1) * 128, :], w_out[rb * 128:(rb + 1) * 128, :]))
        for rb in range(8):
            for c_ in range(0, 2 * DFF, 1024):
                w_ = min(1024, 2 * DFF - c_)
                pc_blocks.append((scr_wu[rb * 128:(rb + 1) * 128, c_:c_ + w_], w_up[rb * 128:(rb + 1) * 128, c_:c_ + w_]))
        PC_PER = 7
        pcn = [0]

        def precast_chunk(ci):
            for (dst_, src_) in pc_blocks[ci * PC_PER:(ci + 1) * PC_PER]:
                S.dma("pool", "pc%d" % (pcn[0] % 8), dst_, src_)
                pcn[0] += 1

        heads = [("m", h) for h in range(8)] + [("f", h) for h in range(8)]

        def chunks_for(i):
            kind, h = heads[i]
            return mla_chunks(h, i % 2) if kind == "m" else fox_chunks(h, i % 2)

        for c in chunks_for(0):
            c()
        for i, (kind, h) in enumerate(heads):
            nxt = chunks_for(i + 1) if i + 1 < len(heads) else []
            if i == 1:
                nxt.append(scan_chunk)
            if i == 0:
                nxt.insert(2, wfox_chunk)
            if 2 <= i <= 13:
                nxt.insert(5, (lambda i=i: precast_chunk(i - 2)))
            if kind == "m":
                attention(h, i % 2, 96, 96.0 ** -0.5, omT, nxt)
            else:
                attention(h, i % 2, 70, 1.0, ofT, nxt)

        wgate = bv(A1, 0, 8 * 2048).rearrange("p (k n) -> p k n", k=8)
        wbm = bv(A1, 16384, 4 * 1024).rearrange("p (k n) -> p k n", k=4)
        wbf = bv(A1, 20480, 4 * 1024).rearrange("p (k n) -> p k n", k=4)
        wout = bv(A1, 24576, 8 * 1024).rearrange("p (k n) -> p k n", k=8)
        mergedT = bv(A1, 32768, 8 * 512).rearrange("p (c t) -> p c t", c=8)
        EB = [[bv(MISC, (i * 4 + k) * 1024, 512, F32) for k in range(4)] for i in range(2)]
        for pi, c_ in enumerate((0, 1024, 512, 1536)):
            S.dma("pool", "wg_%d" % pi, wgate[:, :, c_:c_ + 512], scr_wg[:, c_:c_ + 512].rearrange("(kc p) n -> p kc n", p=128))
        S.dma("pool", "wbm_0", wbm[:, :, :], scr_wb[0:512, :].rearrange("(kc p) n -> p kc n", p=128))
        S.dma("pool", "wbf_0", wbf[:, :, :], scr_wb[512:1024, :].rearrange("(kc p) n -> p kc n", p=128))
        S.dma("pool", "wo_0", wout[:, :, :], scr_wo.rearrange("(kc p) n -> p kc n", p=128))
        load_ln_params(ln_mix_g, ln_mix_b, "mix")
        deferred = []

        def c_iter(g, c):
            c0, n = gcols(g)
            sg1, sg2, m1, m2 = EB[c % 2]
            PG1, PG2, PB1, PB2 = banks[0], banks[1], banks[2], banks[3]
            for kc in range(8):
                S.op("pe", "matmul", out=PG1[:, 0:n], lhsT=wgate[:, kc, c * 128:(c + 1) * 128], rhs=hT[:, kc, c0:c0 + n], start=(kc == 0), stop=(kc == 7))
            S.op("act", "activation", out=sg1[:, 0:n], in_=PG1[:, 0:n], func=AF.Sigmoid, bias=BG(c))
            for kc in range(8):
                S.op("pe", "matmul", out=PG2[:, 0:n], lhsT=wgate[:, kc, 1024 + c * 128:1024 + (c + 1) * 128], rhs=hT[:, kc, c0:c0 + n], start=(kc == 0), stop=(kc == 7))
            S.op("act", "activation", out=sg2[:, 0:n], in_=PG2[:, 0:n], func=AF.Sigmoid, bias=BG(8 + c))
            for kc in range(4):
                S.op("pe", "matmul", out=PB1[:, 0:n], lhsT=wbm[:, kc, c * 128:(c + 1) * 128], rhs=omT[:, kc, c0:c0 + n], start=(kc == 0), stop=(kc == 3))
            S.op("dve", "tensor_tensor", out=m1[:, 0:n], in0=PB1[:, 0:n], in1=sg1[:, 0:n], op=ALU.mult)
            for kc in range(4):
                S.op("pe", "matmul", out=PB2[:, 0:n], lhsT=wbf[:, kc, c * 128:(c + 1) * 128], rhs=ofT[:, kc, c0:c0 + n], start=(kc == 0), stop=(kc == 3))
            S.op("dve", "tensor_tensor", out=m2[:, 0:n], in0=PB2[:, 0:n], in1=sg2[:, 0:n], op=ALU.mult)
            S.op("pool", "tensor_tensor", out=mergedT[:, c, 0:n], in0=m1[:, 0:n], in1=m2[:, 0:n], op=ALU.add)

        c_iters = [[(lambda g=g, c=c: c_iter(g, c)) for c in range(8)] for g in range(5)]
        stE = {}

        def e1(it):
            ti, t = it
            rows = trows(t)
            hb = LNB[:, t % 4, :]
            for cg in range(2):
                PM = banks[4 + cg]
                for c in range(8):
                    S.op("pe", "matmul", out=PM[0:rows, :], lhsT=mergedT[:, c, ti * 128:ti * 128 + rows], rhs=wout[:, c, cg * 512:(cg + 1) * 512], start=(c == 0), stop=(c == 7))
                S.op("dve", "scalar_tensor_tensor", out=hb[0:rows, cg * 512:(cg + 1) * 512], in0=hb[0:rows, cg * 512:(cg + 1) * 512], scalar=ALPHA, in1=PM[0:rows, :], op0=ALU.mult, op1=ALU.add)

        def e2(it):
            stE[it[1]] = ln_s1(LNB[:, it[1] % 4, :], trows(it[1]))

        def e3(it):
            ln_s2(LNB[:, it[1] % 4, :], trows(it[1]), stE[it[1]])

        def e4(it):
            ln_s3(LNB[:, it[1] % 4, :], trows(it[1]))

        def e5(it):
            t = it[1]
            rows = trows(t)
            tb = LNB[:, t % 4, :]
            while len(deferred) > 2:
                deferred.pop(0)()
            hi = ln_s4(tb, rows)
            S.dma("sp", "hs%d" % (t % 4), scr_h2[t * 128:t * 128 + rows, :], tb[0:rows, :])
            deferred.append(lambda t=t, rows=rows, hi=hi: transpose_to_hT(t, rows, hi))

        for g in range(5):
            for t in GT[g]:
                S.dma("sp", "hr%d" % (t % 4), LNB[0:trows(t), t % 4, :], scr_h[t * 128:t * 128 + trows(t), :])
            first = True
            while c_iters[g]:
                c_iters[g].pop(0)()
                if first:
                    first = False
                    while deferred:
                        deferred.pop(0)()
            items = list(enumerate(GT[g]))
            for it in items:
                e1(it)
            stages = [e2, e3, e4, e5]
            n_ = len(items)
            for k in range(n_ + len(stages) - 1):
                for si in range(len(stages) - 1, -1, -1):
                    i_ = k - si
                    if 0 <= i_ < n_:
                        stages[si](items[i_])
                if g + 1 < 5 and len(c_iters[g + 1]) > 1:
                    c_iters[g + 1].pop(0)()


        WU = [bv(A1, i * 4096, 8 * 512).rearrange("p (k n) -> p k n", k=8) for i in range(4)] + [bv(MISC, i * 4096, 8 * 512).rearrange("p (k n) -> p k n", k=8) for i in range(2)]
        actT = bv(A1, 16384, NJ * 512).rearrange("p (j t) -> p j t", j=NJ)
        GO = 30
        Gb = [bv(A1, 27648 + i * 1152, 548, F32) for i in range(2)]
        cvb = [bv(A1, 29952 + i * 1024, 512, F32) for i in range(2)]
        slb = [bv(A1, 32000 + i * 1024, 512, F32) for i in range(2)]
        carry = bv(A1, 34048, NJ * 2, F32).rearrange("p (j c) -> p j c", c=2)
        wdn = bv(A2, 0, NJ * 1024).rearrange("p (j n) -> p j n", j=NJ)
        S.op("dve", "memset", carry[:, :, :], 0.0)
        load_ln_params(ln_ffn_g, ln_ffn_b, "ffn")
        rounds = [(g, j) for g in range(5) for j in range(0, NJ, 4)]

        def issue_round(ri):
            g_, j_ = rounds[ri]
            nj = min(4, NJ - j_)
            rs = (ri % 3) * 2
            S.dma("pool", "wug%d" % (ri % 3), WU[rs][:, :, 0:nj * 128], scr_wu[:, j_ * 128:(j_ + nj) * 128].rearrange("(kc p) n -> p kc n", p=128))
            S.dma("pool", "wuv%d" % (ri % 3), WU[rs + 1][:, :, 0:nj * 128], scr_wu[:, DFF + j_ * 128:DFF + (j_ + nj) * 128].rearrange("(kc p) n -> p kc n", p=128))

        issue_round(0)
        issue_round(1)
        ri = -1
        for g in range(5):
            c0, n = gcols(g)
            for t in GT[g]:
                S.dma("sp", "hr%d" % (t % 4), LNB[0:trows(t), t % 4, :], scr_h2[t * 128:t * 128 + trows(t), :])
            pend_mult = None
            pend_silu = None
            for j in range(NJ):
                jj = j % 4
                if jj == 0:
                    ri += 1
                    if ri + 2 < len(rounds):
                        issue_round(ri + 2)
                    wg_, wv_ = WU[(ri % 3) * 2], WU[(ri % 3) * 2 + 1]
                    if g == 0 and j == 4:
                        while deferred:
                            deferred.pop(0)()
                        S.dma("pool", "wd0", wdn[:, :, :], w_dn.rearrange("(j p) n -> p j n", p=128))
                pj = j % 2
                PUg, PUv = banks[(j % 3) * 2], banks[(j % 3) * 2 + 1]
                for kc in range(8):
                    S.op("pe", "matmul", out=PUg[:, 0:n], lhsT=wg_[:, kc, jj * 128:(jj + 1) * 128], rhs=hT[:, kc, c0:c0 + n], start=(kc == 0), stop=(kc == 7))
                for kc in range(8):
                    S.op("pe", "matmul", out=PUv[:, 0:n], lhsT=wv_[:, kc, jj * 128:(jj + 1) * 128], rhs=hT[:, kc, c0:c0 + n], start=(kc == 0), stop=(kc == 7))
                G_ = Gb[pj]
                cv = cvb[pj]
                sl = slb[pj]
                S.op("act", "activation", out=G_[:, GO + 2:GO + 2 + n], in_=PUg[:, 0:n], func=AF.Copy)
                S.op("act", "activation", out=G_[:, GO:GO + 2], in_=carry[:, j, :], func=AF.Copy)
                if pend_silu is not None:
                    pend_silu()
                S.op("dve", "tensor_scalar", out=cv[:, 0:n], in0=G_[:, GO + 2:GO + 2 + n], scalar1=CW(2, j), scalar2=CB(j), op0=ALU.mult, op1=ALU.add)
                S.op("dve", "scalar_tensor_tensor", out=cv[:, 0:n], in0=G_[:, GO + 1:GO + 1 + n], scalar=CW(1, j), in1=cv[:, 0:n], op0=ALU.mult, op1=ALU.add)
                S.op("dve", "scalar_tensor_tensor", out=cv[:, 0:n], in0=G_[:, GO:GO + n], scalar=CW(0, j), in1=cv[:, 0:n], op0=ALU.mult, op1=ALU.add)
                S.op("act", "activation", out=carry[:, j, :], in_=G_[:, GO + n:GO + n + 2], func=AF.Copy)
                if pend_mult is not None:
                    pend_mult()
                pend_silu = (lambda cv=cv, sl=sl: S.op("act", "activation", out=sl[:, 0:n], in_=cv[:, 0:n], func=AF.Silu))
                pend_mult = (lambda j=j, PUv=PUv, sl=sl: S.op("dve", "tensor_tensor", out=actT[:, j, 0:n], in0=PUv[:, 0:n], in1=sl[:, 0:n], op=ALU.mult))
            pend_silu()
            pend_mult()
            stF = {}

            def f1(it):
                ti, t = it
                rows = trows(t)
                hb = LNB[:, t % 4, :]
                for cg in range(2):
                    PD = banks[4 + (ti % 2) * 2 + cg]
                    for j in range(NJ):
                        S.op("pe", "matmul", out=PD[0:rows, :], lhsT=actT[:, j, ti * 128:ti * 128 + rows], rhs=wdn[:, j, cg * 512:(cg + 1) * 512], start=(j == 0), stop=(j == NJ - 1))
                    S.op("dve", "scalar_tensor_tensor", out=hb[0:rows, cg * 512:(cg + 1) * 512], in0=hb[0:rows, cg * 512:(cg + 1) * 512], scalar=ALPHA, in1=PD[0:rows, :], op0=ALU.mult, op1=ALU.add)

            def f2(it):
                stF[it[1]] = ln_s1(LNB[:, it[1] % 4, :], trows(it[1]))

            def f3(it):
                ln_s2(LNB[:, it[1] % 4, :], trows(it[1]), stF[it[1]])

            def f4(it):
                ln_s3(LNB[:, it[1] % 4, :], trows(it[1]))

            def f5(it):
                t = it[1]
                rows = trows(t)
                ob = LNB[:, t % 4, :]
                ln_s4(ob, rows, want16=False)
                if t == 0:
                    S.dma("sp", "o%d" % (t % 4), out[0:112, :], ob[16:128, :])
                else:
                    S.dma("sp", "o%d" % (t % 4), out[t * 128 - 16:t * 128 - 16 + rows, :], ob[0:rows, :])

            run_skewed([f1, f2, f3, f4, f5], list(enumerate(GT[g])))

        S.emit(final_dma_keys=["o0", "o1", "o2", "o3"])
    return nc


_CACHE = {}


def _consts():
    ident = np.eye(128, dtype=np.float32)
    k = np.arange(128)[:, None]
    q = np.arange(128)[None, :]
    mask = np.where(q >= k, 0.0, -30000.0).astype(np.float32)
    half = 16
    inv_freq = (np.float32(10000.0) ** (-np.arange(half, dtype=np.float32) / np.float32(half))).astype(np.float32)
    pos = np.arange(L, dtype=np.float32)
    ang = (pos[None, :] * inv_freq[:, None]).astype(np.float32)
    cos = np.cos(ang).astype(np.float32)
    sin = np.sin(ang).astype(np.float32)
    c_cos = np.concatenate([cos, cos], axis=0)
    c_sin = np.concatenate([-sin, sin], axis=0)
    return ident, mask, np.ascontiguousarray(c_cos), np.ascontiguousarray(c_sin)


def kernel(**inputs):
    if "nc" not in _CACHE:
        _CACHE["nc"] = build_program()
    nc = _CACHE["nc"]
    f = lambda a: np.ascontiguousarray(np.asarray(a, dtype=np.float32))
    ident, mask, c_cos, c_sin = _consts()
    shared = {
        "meta_tokens": f(inputs["meta_tokens"]),
        "ln_emb_g": f(inputs["ln_emb_g"]), "ln_emb_b": f(inputs["ln_emb_b"]),
        "w_in": f(inputs["w_in"])[0], "b_gate": f(inputs["b_gate"])[0], "b_forget": f(inputs["b_forget"])[0],
        "q_norm_g": f(inputs["q_norm_g"])[0], "w_q_up": f(inputs["w_q_up"])[0],
        "kv_norm_g": f(inputs["kv_norm_g"])[0], "w_kv_up": f(inputs["w_kv_up"])[0],
        "w_branch_mla": f(inputs["w_branch_mla"])[0], "w_branch_fox": f(inputs["w_branch_fox"])[0],
        "w_out": f(inputs["w_out"])[0],
        "ln_mix_g": f(inputs["ln_mix_g"])[0], "ln_mix_b": f(inputs["ln_mix_b"])[0],
        "w_ffn_up": f(inputs["w_ffn_up"])[0], "conv_w": f(inputs["conv_w"])[0], "conv_b": f(inputs["conv_b"])[0],
        "w_ffn_down": f(inputs["w_ffn_down"])[0],
        "ln_ffn_g": f(inputs["ln_ffn_g"])[0], "ln_ffn_b": f(inputs["ln_ffn_b"])[0],
        "c_ident": ident, "c_mask": mask, "c_cos": c_cos, "c_sin": c_sin,
    }
    x = f(inputs["x"])
    in_maps = []
    for b in range(8):
        m = dict(shared)
        m["x"] = x[b]
        in_maps.append(m)
    res = run_bass_kernel_spmd(nc, in_maps, core_ids=list(range(8)))
    return np.stack([np.asarray(r["out"], dtype=np.float32) for r in res.results], axis=0)
```

```python
import numpy as np
import concourse.bass as bass
import concourse.mybir as mybir
from concourse.bass_utils import run_bass_kernel_spmd

F32 = mybir.dt.float32
BF16 = mybir.dt.bfloat16
AF = mybir.ActivationFunctionType
ALU = mybir.AluOpType
AX = mybir.AxisListType


class _Op:
    __slots__ = ("eng", "stream", "pos", "fn", "waits", "clock", "signal", "sigval", "dma")

    def __init__(self, eng, stream, pos, fn, dma):
        self.eng = eng
        self.stream = stream
        self.pos = pos
        self.fn = fn
        self.waits = []
        self.clock = None
        self.signal = dma
        self.sigval = 0
        self.dma = dma


class Sched:
    ENGS = ("pe", "act", "dve", "pool", "sp")
    G = 128

    def __init__(self, nc):
        self.nc = nc
        self.eops = {e: [] for e in self.ENGS}
        self.cstream = {e: [] for e in self.ENGS}
        self.dstream = {}
        self.eclock = {e: {} for e in self.ENGS}
        self.state = {}
        self._gcache = {}

    def _keys(self, ap):
        t = ap.tensor
        name = t.name
        dims = [tuple(d) for d in ap.ap]
        ck = (name, ap.offset, tuple(dims), str(ap.dtype))
        r = self._gcache.get(ck)
        if r is not None:
            return r
        space = str(ap.space)
        if "PSUM" in space.upper():
            pstride, pcount = dims[0]
            p0 = ap.offset // pstride if pstride > 0 else 0
            r = tuple(("PSUM", name, q) for q in range(p0 // 32, (p0 + pcount - 1) // 32 + 1))
            self._gcache[ck] = r
            return r
        n = ap.size()
        esz = ap.nbytes() // max(1, n) if n else 4
        if esz == 0:
            esz = 4
        isdram = "DRAM" in space.upper() or "HBM" in space.upper()
        if isdram:
            fdims = dims
            foff = ap.offset
            qs = (0,)
            G = 4096
        else:
            pstride, pcount = dims[0]
            fdims = dims[1:]
            p0 = ap.offset // pstride if pstride > 0 else 0
            foff = ap.offset - p0 * pstride
            qs = tuple(range(p0 // 32, (p0 + pcount - 1) // 32 + 1))
            G = self.G
        fdims = [d for d in fdims if d[1] > 1 and d[0] != 0]
        if not fdims:
            ranges = [(foff, foff + 1)]
        else:
            fd = sorted(fdims, key=lambda d: abs(d[0]))
            s0, n0 = fd[0]
            outer = fd[1:]
            cnt = 1
            for d in outer:
                cnt *= d[1]
            if cnt > 256:
                lo = foff
                hi = foff + sum((d[1] - 1) * d[0] for d in fd) + 1
                ranges = [(lo, hi)]
            else:
                starts = [foff]
                for (s, c) in outer:
                    starts = [b + i * s for b in starts for i in range(c)]
                ln = (n0 - 1) * s0 + 1
                ranges = [(b, b + ln) for b in starts]
        gs = set()
        for lo, hi in ranges:
            for g in range((lo * esz) // G, (hi * esz - 1) // G + 1):
                gs.add(g)
        r = tuple((name, q, g) for q in qs for g in gs)
        self._gcache[ck] = r
        return r

    def _add(self, eng, fn, reads, writes, dma_key=None):
        dma = dma_key is not None
        if dma:
            lst = self.dstream.setdefault(dma_key, [])
            stream = "d:" + dma_key
            op = _Op(eng, stream, len(lst) + 1, fn, True)
        else:
            lst = self.cstream[eng]
            stream = eng
            op = _Op(eng, stream, len(lst) + 1, fn, False)
        deps = set()
        if dma and lst:
            deps.add(lst[-1])
        rk = []
        wk = []
        pk = []
        for ap in reads:
            for k in self._keys(ap):
                (pk if k[0] == "PSUM" else rk).append(k)
        for ap in writes:
            wk.extend(self._keys(ap))
        st = self.state
        for k in rk:
            s = st.get(k)
            if s is not None and s[0] is not None:
                deps.add(s[0])
        for k in pk:
            s = st.get(k)
            if s is not None:
                if s[0] is not None:
                    deps.add(s[0])
                for rs_, ro_ in s[1].items():
                    if rs_ != stream:
                        deps.add(ro_)
        for k in wk:
            s = st.get(k)
            if s is not None:
                if s[0] is not None:
                    deps.add(s[0])
                deps.update(s[1].values())
        clk = self.eclock[eng]
        for d in sorted(deps, key=lambda d: -d.pos):
            if d.stream == "pe" and eng == "pe" and not dma:
                continue
            if clk.get(d.stream, 0) >= d.pos:
                continue
            op.waits.append((d.stream, d.pos))
            d.signal = True
            for s, p in d.clock.items():
                if clk.get(s, 0) < p:
                    clk[s] = p
        c = dict(clk)
        c[stream] = op.pos
        op.clock = c
        for k in rk + pk:
            s = st.get(k)
            if s is None:
                st[k] = [None, {stream: op}]
            else:
                s[1][stream] = op
        for k in wk:
            st[k] = [op, {}]
        lst.append(op)
        self.eops[eng].append(op)
        return op

    def pe(self, fn, r=(), w=()):
        return self._add("pe", fn, r, w)

    def act(self, fn, r=(), w=()):
        return self._add("act", fn, r, w)

    def dve(self, fn, r=(), w=()):
        return self._add("dve", fn, r, w)

    def pool(self, fn, r=(), w=()):
        return self._add("pool", fn, r, w)

    def op(self, eng, name, *args, **kw):
        wr = [kw[k] for k in ("out", "accum_out") if isinstance(kw.get(k), bass.AP)]
        rd = [v for k, v in kw.items() if k not in ("out", "accum_out") and isinstance(v, bass.AP)]
        if name == "memset":
            wr = [args[0]]
        return self._add(eng, lambda e: getattr(e, name)(*args, **kw), rd, wr)

    def dma(self, eng, key, out, in_, **kw):
        return self._add(eng, lambda e: e.dma_start(out=out, in_=in_, **kw), [in_], [out], dma_key=key)

    def barrier_all(self, aps_by_eng=None):
        last = {}
        for e in self.ENGS:
            if self.cstream[e]:
                last[e] = self.cstream[e][-1]
        for k, lst in self.dstream.items():
            if lst:
                last["d:" + k] = lst[-1]
        for e in ("pe", "act", "dve", "pool", "sp"):
            clk = self.eclock[e]
            waits = []
            for s, d in last.items():
                if clk.get(s, 0) >= d.pos:
                    continue
                waits.append((s, d.pos))
                d.signal = True
                for s2, p in d.clock.items():
                    if clk.get(s2, 0) < p:
                        clk[s2] = p
            if waits:
                op = _Op(e, "nop", 0, None, False)
                op.waits = waits
                op.clock = dict(clk)
                self.eops[e].append(op)

    def emit(self, final_dma_keys=()):
        nc = self.nc
        for e in self.ENGS:
            cnt = 0
            for op in self.cstream[e]:
                if op.signal:
                    cnt += 1
                    op.sigval = cnt
        import contextlib
        with contextlib.ExitStack() as es:
            sems = {}
            for e in self.ENGS:
                if any(op.signal for op in self.cstream[e]):
                    sems[e] = es.enter_context(nc.semaphore("s_" + e))
            for k in self.dstream:
                sems["d:" + k] = es.enter_context(nc.semaphore("sd_" + k))
            self.nsems = len(sems)
            block = es.enter_context(nc.Block())

            def run(engname):
                def body(eng):
                    for op in self.eops[engname]:
                        for (s, p) in op.waits:
                            if s.startswith("d:"):
                                v = 16 * p
                            else:
                                v = self.cstream[s][p - 1].sigval
                                assert v > 0
                            eng.wait_ge(sems[s], v)
                        if op.fn is None:
                            continue
                        ins = op.fn(eng)
                        if op.dma:
                            ins.then_inc(sems[op.stream], 16)
                        elif op.signal:
                            ins.then_inc(sems[op.stream], 1)
                    if engname == "sp":
                        for k in final_dma_keys:
                            lst = self.dstream.get(k)
                            if lst:
                                eng.wait_ge(sems["d:" + k], 16 * len(lst))
                return body

            block.tensor(run("pe"))
            block.scalar(run("act"))
            block.vector(run("dve"))
            block.gpsimd(run("pool"))
            block.sync(run("sp"))


D = 1024
SEQ = 2048
NMETA = 16
L = SEQ + NMETA
NT = 17
GT = ((0, 1, 2, 3), (4, 5, 6, 7), (8, 9, 10), (11, 12, 13), (14, 15, 16))
DFF = 2816
NJ = 22
LN_EPS = 1e-5
RMS_EPS = 1e-6
ALPHA = 2.0 ** 0.25
INTOT = 4136
O_QLAT, O_KVLAT, O_KROPE, O_FQ, O_FK, O_FV, O_FLOG, O_GATE = 0, 384, 512, 544, 1056, 1568, 2080, 2088


def trows(t):
    return 128 if t < 16 else 16


def gcols(g):
    c0 = GT[g][0] * 128
    n = sum(trows(t) for t in GT[g])
    return c0, n


def build_program():
    nc = bass.Bass("TRN2", target_bir_lowering=False)
    dt_in = lambda n, s: nc.dram_tensor(n, s, F32, kind="ExternalInput").ap()
    x = dt_in("x", [SEQ, D])
    meta = dt_in("meta_tokens", [NMETA, D])
    ln_emb_g = dt_in("ln_emb_g", [D]); ln_emb_b = dt_in("ln_emb_b", [D])
    w_in = dt_in("w_in", [D, INTOT])
    b_gate = dt_in("b_gate", [2 * D]); b_forget = dt_in("b_forget", [8])
    q_norm_g = dt_in("q_norm_g", [384]); w_q_up = dt_in("w_q_up", [384, 768])
    kv_norm_g = dt_in("kv_norm_g", [128]); w_kv_up = dt_in("w_kv_up", [128, 1024])
    w_bm = dt_in("w_branch_mla", [512, D]); w_bf = dt_in("w_branch_fox", [512, D])
    w_out = dt_in("w_out", [D, D])
    ln_mix_g = dt_in("ln_mix_g", [D]); ln_mix_b = dt_in("ln_mix_b", [D])
    w_up = dt_in("w_ffn_up", [D, 2 * DFF])
    conv_w = dt_in("conv_w", [3, DFF]); conv_b = dt_in("conv_b", [DFF])
    w_dn = dt_in("w_ffn_down", [DFF, D])
    ln_ffn_g = dt_in("ln_ffn_g", [D]); ln_ffn_b = dt_in("ln_ffn_b", [D])
    c_ident = dt_in("c_ident", [128, 128]); c_mask = dt_in("c_mask", [128, 128])
    c_cos = dt_in("c_cos", [32, L]); c_sin = dt_in("c_sin", [32, L])
    out = nc.dram_tensor("out", [SEQ, D], F32, kind="ExternalOutput").ap()
    scr_h = nc.dram_tensor("scr_h", [NT * 128, D], F32, kind="Internal").ap()
    scr_h2 = nc.dram_tensor("scr_h2", [NT * 128, D], F32, kind="Internal").ap()
    scr_wu = nc.dram_tensor("scr_wu", [D, 2 * DFF], BF16, kind="Internal").ap()
    scr_wg = nc.dram_tensor("scr_wg", [D, 2 * D], BF16, kind="Internal").ap()
    scr_wb = nc.dram_tensor("scr_wb", [D, D], BF16, kind="Internal").ap()
    scr_wo = nc.dram_tensor("scr_wo", [D, D], BF16, kind="Internal").ap()

    import contextlib
    with contextlib.ExitStack() as es:
        sbt = lambda n, s, d: es.enter_context(nc.sbuf_tensor(n, s, d))
        LP = 2112
        hT = sbt("hT", [128, 8, LP], BF16)
        A1 = sbt("A1", [128, 36864], BF16)
        A2 = sbt("A2", [128, 22528], BF16)
        MISC = sbt("MISC", [128, 10320], BF16)
        LNB = sbt("LNB", [128, 4, 1024], F32)
        GB = sbt("GB", [128, 2, 1024], F32)
        H16 = sbt("H16", [128, 2, 1024], BF16)
        SM = sbt("SM", [128, 256], F32)
        PP = sbt("PP", [128, 128], F32)
        CST = sbt("CST", [128, 2, 128], BF16)
        WK = sbt("WK", [128, 1536], BF16)
        CSTF = A2[:, 4096:4608].bitcast(F32).rearrange("p (a b) -> p a b", a=2)
        IDF = A2[:, 4608:4864].bitcast(F32)
        PPS = A2[:, 4864:5120].bitcast(F32)
        banks = [es.enter_context(nc.psum_tensor("PS%d" % i, [128, 512], F32)) for i in range(8)]
        S = Sched(nc)

        def bv(arena, off, n, dt=BF16):
            if dt == BF16:
                return arena[:, off:off + n]
            return arena[:, off:off + 2 * n].bitcast(F32)

        identb = CST[:, 0, :]
        maskb = CST[:, 1, :]
        PSB = [b[:, :].bitcast(BF16) for b in banks]

        cosT = bv(MISC, 0, L, F32)
        sinT = bv(MISC, 2 * L, L, F32)
        kpeT = bv(MISC, 4 * L, L)
        efb = bv(MISC, 0, L, F32)
        cparts = bv(MISC, 2 * L, 3 * L).rearrange("p (c t) -> p c t", c=3)
        BG = lambda c: PP[:, c:c + 1]
        CW = lambda k, j: PP[:, 16 + k * 22 + j:16 + k * 22 + j + 1]
        CB = lambda j: PP[:, 82 + j:83 + j]
        NH = sbt("NH", [128, 2], F32)
        BF8 = sbt("BF8", [128, 2], F32)
        NEGH = NH[:, 0:2]
        NEGBF = BF8[0:8, 1:2]
        S.op("dve", "memset", NH[:, :], -0.5)

        def consts1():
            S.dma("sp", "c0", CSTF[:, 0, :], c_ident)
            S.dma("sp", "c1", CSTF[:, 1, :], c_mask)
            S.op("dve", "tensor_copy", out=CST[:], in_=CSTF[:])
            S.dma("sp", "c9", GQ[:, 0:384], q_norm_g.partition_broadcast(128))
            S.dma("sp", "c10", GQ[:, 384:512], kv_norm_g.partition_broadcast(128))

        def consts2():
            S.dma("sp", "c3", cosT[64:96, :], c_cos)
            S.dma("sp", "c4", sinT[64:96, :], c_sin)
            S.dma("sp", "c8", BF8[0:8, 0:1], b_forget.unsqueeze(1))
            S.op("dve", "tensor_scalar", out=BF8[0:8, 1:2], in0=BF8[0:8, 0:1], scalar1=-1.0, scalar2=None, op0=ALU.mult)

        def consts3():
            S.dma("sp", "c2", IDF[:], c_ident)
            S.dma("sp", "c5", PPS[0:16, :], b_gate.rearrange("(c p) -> c p", p=128))
            S.dma("sp", "c6", PPS[16:82, :], conv_w.rearrange("k (j p) -> (k j) p", p=128))
            S.dma("sp", "c7", PPS[82:104, :], conv_b.rearrange("(j p) -> j p", p=128))
            S.op("pe", "transpose", out=banks[7][:, 0:104], in_=PPS[0:104, :], identity=IDF[0:104, 0:104])
            S.op("act", "activation", out=PP[:, 0:104], in_=banks[7][:, 0:104], func=AF.Copy)

        smi = [0]

        def sm(n):
            o = (smi[0] % 8) * 32
            smi[0] += 1
            return SM[:, o:o + n]

        def load_ln_params(g_ap, b_ap, tag):
            S.dma("sp", "lng", GB[:, 0, :], g_ap.partition_broadcast(128))
            S.dma("sp", "lnb", GB[:, 1, :], b_ap.partition_broadcast(128))

        h16i = [0]
        SM2 = sbt("SM2", [128, 128], F32)

        def ln_s1(buf, rows):
            s = sm(16)
            S.op("dve", "bn_stats", out=s[0:rows, 0:6], in_=buf[0:rows, 0:512])
            S.op("dve", "bn_stats", out=s[0:rows, 6:12], in_=buf[0:rows, 512:1024])
            S.op("dve", "bn_aggr", out=s[0:rows, 12:14], in_=s[0:rows, 0:12])
            S.op("dve", "tensor_scalar", out=s[0:rows, 14:15], in0=s[0:rows, 13:14], scalar1=LN_EPS, scalar2=None, op0=ALU.add)
            S.op("pool", "tensor_tensor", out=s[0:rows, 15:16], in0=s[0:rows, 14:15], in1=NEGH[0:rows, 0:1], op=ALU.pow)
            return s

        def ln_s2(buf, rows, s):
            S.op("dve", "tensor_scalar", out=s[0:rows, 14:15], in0=s[0:rows, 12:13], scalar1=s[0:rows, 15:16], scalar2=-1.0, op0=ALU.mult, op1=ALU.mult)
            S.op("act", "activation", out=buf[0:rows, :], in_=buf[0:rows, :], func=AF.Identity, scale=s[0:rows, 15:16], bias=s[0:rows, 14:15])

        def ln_s3(buf, rows):
            S.op("pool", "tensor_tensor", out=buf[0:rows, :], in0=buf[0:rows, :], in1=GB[0:rows, 0, :], op=ALU.mult)

        def ln_s4(buf, rows, want16=True):
            S.op("dve", "tensor_tensor", out=buf[0:rows, :], in0=buf[0:rows, :], in1=GB[0:rows, 1, :], op=ALU.add)
            if want16:
                h16i[0] += 1
                S.op("act", "activation", out=H16[0:rows, h16i[0] % 2, :], in_=buf[0:rows, :], func=AF.Copy)
                return h16i[0] % 2

        def layernorm(buf, rows, out32=None, want16=True):
            s = ln_s1(buf, rows)
            ln_s2(buf, rows, s)
            ln_s3(buf, rows)
            return ln_s4(buf, rows, want16)

        tri = [0]

        def transpose_to_hT(t, rows, hi):
            pb = PSB[6 + (tri[0] % 2)]
            tri[0] += 1
            for c in range(8):
                S.op("pe", "transpose", out=pb[:, c * 128:c * 128 + rows], in_=H16[0:rows, hi, c * 128:(c + 1) * 128], identity=identb[0:rows, 0:rows])
            S.op("act", "activation", out=hT[:, :, t * 128:t * 128 + rows],
                 in_=pb[:, :].rearrange("p (c t) -> p c t", c=8)[:, :, 0:rows], func=AF.Copy)

        def wload(key, dst3, src2, col0, ncols):
            c = 0
            i = 0
            while c < ncols:
                w = min(1024, ncols - c)
                S.dma("pool", "%s_%d" % (key, i % 2), dst3[:, :, c:c + w],
                      src2[:, col0 + c:col0 + c + w].rearrange("(kc p) n -> p kc n", p=128))
                c += w
                i += 1

        latT = bv(A1, 0, 4 * LP).rearrange("p (c t) -> p c t", c=4)
        QK = [bv(A1, 8448 + i * LP, LP) for i in range(4)]
        VA = [bv(A1, 16896 + i * 2176, 2176).rearrange("p (t c) -> p t c", c=128) for i in range(2)]
        wfox = bv(A1, 21248, 8 * 1536).rearrange("p (k n) -> p k n", k=8)
        wlat = bv(A1, 21248, 8 * 512).rearrange("p (k n) -> p k n", k=8)
        wkr = bv(A1, 25344, 8 * 192).rearrange("p (k n) -> p k n", k=8)
        wfl = bv(A1, 26880, 8 * 8).rearrange("p (k n) -> p k n", k=8)
        latn = [bv(A1, 26944 + i * 512, 512) for i in range(2)]
        GQ = bv(A1, 27968, 512, F32)
        omT = bv(A2, 0, 4 * L).rearrange("p (c t) -> p c t", c=4)
        ofT = bv(A2, 8256, 4 * L).rearrange("p (c t) -> p c t", c=4)
        wq = bv(A2, 16512, 3 * 768).rearrange("p (k n) -> p k n", k=3)
        wqB = bv(A2, 18816, 3 * 768).rearrange("p (k h d) -> p k h d", k=3, h=8)
        wkv = bv(A2, 21120, 1024)
        PT = [bv(WK, i * 512, 512) for i in range(3)]
        rt1 = LNB[:, 0, 0:512]
        rt2 = LNB[:, 1, 0:512]
        sbs = [LNB[:, 2, 0:512], LNB[:, 3, 0:512]]
        RT = bv(A2, 0, 1024, F32).rearrange("p (a b) -> p a b", a=2)
        junk = bv(A2, 2048, 512)
        junk2 = bv(A2, 2560, 512)

        load_ln_params(ln_emb_g, ln_emb_b, "emb")
        wload("wl", wlat, w_in, 0, 512)
        def prep1():
            S.op("pool", "memset", wkr[:, :, :], 0.0)
            S.dma("pool", "wk0", wkr[:, :, 64:96], w_in[:, O_KROPE:O_KROPE + 32].rearrange("(kc p) n -> p kc n", p=128))
            S.dma("pool", "wk1", wkr[:, :, 160:176], w_in[:, O_KROPE + 16:O_KROPE + 32].rearrange("(kc p) n -> p kc n", p=128))
            S.dma("pool", "wk2", wkr[:, :, 176:192], w_in[:, O_KROPE:O_KROPE + 16].rearrange("(kc p) n -> p kc n", p=128))
            S.dma("pool", "wk3", wfl[:, :, :], w_in[:, O_FLOG:O_FLOG + 8].rearrange("(kc p) n -> p kc n", p=128))

        def prep2():
            S.dma("pool", "wq", wq[:, :, :], w_q_up.rearrange("(kc p) n -> p kc n", p=128))
            S.dma("pool", "wkv", wkv[:, :], w_kv_up)

        def prep3():
            S.op("pool", "memset", wqB[:, :, :, :], 0.0)
            wq4 = wq.rearrange("p k (h d) -> p k h d", h=8)
            S.op("pool", "tensor_copy", out=wqB[:, :, :, 64:80], in_=wq4[:, :, :, 80:96])
            S.op("pool", "tensor_copy", out=wqB[:, :, :, 80:96], in_=wq4[:, :, :, 64:80])

        hooksAB = {1: consts1, 2: prep1, 3: consts2, 5: prep2, 8: prep3, 12: consts3}

        def rope(PA, PB, dst, c0, n, t1=None, t2=None):
            t1 = rt1 if t1 is None else t1
            t2 = rt2 if t2 is None else t2
            S.op("dve", "tensor_tensor", out=t1[64:96, 0:n], in0=PA[64:96, 0:n], in1=cosT[64:96, c0:c0 + n], op=ALU.mult)
            S.op("dve", "tensor_tensor", out=t2[64:96, 0:n], in0=PB[64:96, 0:n], in1=sinT[64:96, c0:c0 + n], op=ALU.mult)
            S.op("dve", "tensor_tensor", out=dst[64:96, c0:c0 + n], in0=t1[64:96, 0:n], in1=t2[64:96, 0:n], op=ALU.add)

        st = {}

        XS = [LNB[:, i, :] for i in range(4)] + [bv(A2, 6144 + i * 2048, 1024, F32) for i in range(2)]

        def sA0(t):
            rows = trows(t)
            xb = XS[t % 6]
            if t == 0:
                S.dma("sp", "xm", xb[0:16, :], meta)
                S.dma("sp", "x0", xb[16:128, :], x[0:112, :])
            else:
                S.dma("sp", "x%d" % (t % 6), xb[0:rows, :], x[t * 128 - 16:t * 128 - 16 + rows, :])

        def sA1(t):
            st[t] = ln_s1(XS[t % 6], trows(t))

        def sA2(t):
            ln_s2(XS[t % 6], trows(t), st[t])

        def sA3(t):
            ln_s3(XS[t % 6], trows(t))

        def sA4(t):
            rows = trows(t)
            hb = XS[t % 6]
            st[t] = ln_s4(hb, rows)
            S.dma("sp", "hs%d" % (t % 6), scr_h[t * 128:t * 128 + rows, :], hb[0:rows, :])

        def sA5(t):
            rows = trows(t)
            transpose_to_hT(t, rows, st[t])
            for g in range(5):
                if GT[g][-1] == t:
                    c0, n = gcols(g)
                    PA, PBk = banks[4], banks[5]
                    for kc in range(8):
                        S.op("pe", "matmul", out=PA[0:96, 0:n], lhsT=wkr[:, kc, 0:96], rhs=hT[:, kc, c0:c0 + n], start=(kc == 0), stop=(kc == 7))
                    for kc in range(8):
                        S.op("pe", "matmul", out=PBk[0:96, 0:n], lhsT=wkr[:, kc, 96:192], rhs=hT[:, kc, c0:c0 + n], start=(kc == 0), stop=(kc == 7))
                    rope(PA, PBk, kpeT, c0, n, RT[:, 0, :], RT[:, 1, :])
                    PF = banks[4]
                    for kc in range(8):
                        S.op("pe", "matmul", out=PF[0:8, 0:n], lhsT=wfl[:, kc, 0:8], rhs=hT[:, kc, c0:c0 + n], start=(kc == 0), stop=(kc == 7))
                    S.op("act", "activation", out=efb[0:8, c0:c0 + n], in_=PF[0:8, 0:n], func=AF.Exp, scale=-1.0, bias=NEGBF)
                    S.op("act", "activation", out=efb[0:8, c0:c0 + n], in_=efb[0:8, c0:c0 + n], func=AF.Ln, bias=1.0)

        def sB6(t):
            rows = trows(t)
            PL = banks[t % 3]
            for kc in range(8):
                S.op("pe", "matmul", out=PL[0:rows, :], lhsT=hT[:, kc, t * 128:t * 128 + rows], rhs=wlat[:, kc, :], start=(kc == 0), stop=(kc == 7))
            s_ = SM2[:, (t % 4) * 32:(t % 4) * 32 + 8]
            jk = junk if t % 2 == 0 else junk2
            S.op("act", "activation", out=jk[0:rows, 0:384], in_=PL[0:rows, 0:384], func=AF.Square, accum_out=s_[0:rows, 0:1])
            S.op("act", "activation", out=jk[0:rows, 384:512], in_=PL[0:rows, 384:512], func=AF.Square, accum_out=s_[0:rows, 1:2])

        def sB7(t):
            rows = trows(t)
            PL = banks[t % 3]
            s_ = SM2[:, (t % 4) * 32:(t % 4) * 32 + 8]
            S.op("dve", "tensor_scalar", out=s_[0:rows, 2:3], in0=s_[0:rows, 0:1], scalar1=1.0 / 384, scalar2=RMS_EPS, op0=ALU.mult, op1=ALU.add)
            S.op("dve", "tensor_scalar", out=s_[0:rows, 3:4], in0=s_[0:rows, 1:2], scalar1=1.0 / 128, scalar2=RMS_EPS, op0=ALU.mult, op1=ALU.add)
            S.op("pool", "tensor_tensor", out=s_[0:rows, 4:6], in0=s_[0:rows, 2:4], in1=NEGH[0:rows, 0:2], op=ALU.pow)

        def sB8(t):
            rows = trows(t)
            PL = banks[t % 3]
            s_ = SM2[:, (t % 4) * 32:(t % 4) * 32 + 8]
            ln_ = latn[t % 2]
            S.op("dve", "scalar_tensor_tensor", out=ln_[0:rows, 0:384], in0=PL[0:rows, 0:384], scalar=s_[0:rows, 4:5], in1=GQ[0:rows, 0:384], op0=ALU.mult, op1=ALU.mult)
            S.op("dve", "scalar_tensor_tensor", out=ln_[0:rows, 384:512], in0=PL[0:rows, 384:512], scalar=s_[0:rows, 5:6], in1=GQ[0:rows, 384:512], op0=ALU.mult, op1=ALU.mult)

        def sB9(t):
            rows = trows(t)
            ln_ = latn[t % 2]
            pb = PSB[3]
            for c in range(4):
                S.op("pe", "transpose", out=pb[:, c * 128:c * 128 + rows], in_=ln_[0:rows, c * 128:(c + 1) * 128], identity=identb[0:rows, 0:rows])
            S.op("act", "activation", out=latT[:, :, t * 128:t * 128 + rows],
                 in_=pb[:, 0:512].rearrange("p (c t) -> p c t", c=4)[:, :, 0:rows], func=AF.Copy)

        def run_skewed(stages, items):
            n_ = len(items)
            for k in range(n_ + len(stages) - 1):
                for si in range(len(stages) - 1, -1, -1):
                    i_ = k - si
                    if 0 <= i_ < n_:
                        stages[si](items[i_])

        stagesAB = [sA0, sA1, sA2, sA3, sA4, sA5, sB6, sB7, sB8, sB9]
        drain_parts = {}

        def run_AB():
            for k in range(NT + len(stagesAB) - 1):
                if k in hooksAB:
                    hooksAB[k]()
                for si in (9, 8, 7, 5, 6, 4, 3, 2, 1, 0):
                    t = k - si
                    if 0 <= t < NT:
                        stagesAB[si](t)
                for p_ in drain_parts.get(k, []):
                    p_()

        def scan_chunk():
            onesb = bv(A2, 8256, 1024, F32)
            r1b = bv(A2, 8256 + 2048, L, F32)
            S.op("dve", "memset", onesb[0:8, :], 1.0)
            prev = None
            for a in range(0, L, 1024):
                w = min(1024, L - a)
                S.op("dve", "tensor_tensor_scan", out=efb[0:8, a:a + w], data0=onesb[0:8, 0:w], data1=efb[0:8, a:a + w],
                     initial=(0.0 if prev is None else prev), op0=ALU.mult, op1=ALU.add)
                prev = efb[0:8, a + w - 1:a + w]
            S.op("dve", "tensor_copy", out=cparts[0:8, 0, :], in_=efb[0:8, :])
            S.op("dve", "tensor_tensor", out=r1b[0:8, :], in0=efb[0:8, :], in1=cparts[0:8, 0, :], op=ALU.subtract)
            S.op("dve", "tensor_copy", out=cparts[0:8, 1, :], in_=r1b[0:8, :])
            S.op("dve", "tensor_tensor", out=r1b[0:8, :], in0=r1b[0:8, :], in1=cparts[0:8, 1, :], op=ALU.subtract)
            S.op("dve", "tensor_copy", out=cparts[0:8, 2, :], in_=r1b[0:8, :])


        def wfox_chunk():
            wfqk = wfox[:, :, 0:1024].rearrange("p k (h a c) -> p k h a c", h=8, a=2)
            for kc in range(8):
                S.dma("pool", "wfq%d" % kc, wfqk[:, kc, :, 0, :], w_in[kc * 128:(kc + 1) * 128, O_FQ:O_FQ + 512].rearrange("p (h c) -> p h c", h=8))
                S.dma("pool", "wfk%d" % kc, wfqk[:, kc, :, 1, :], w_in[kc * 128:(kc + 1) * 128, O_FK:O_FK + 512].rearrange("p (h c) -> p h c", h=8))
            S.dma("pool", "wf_2", wfox[:, :, 1024:1536], w_in[:, O_FV:O_FV + 512].rearrange("(kc p) n -> p kc n", p=128))

        ring = [0]

        def nbank():
            b = banks[ring[0] % 4]
            ring[0] += 1
            return b

        PTR = PT + [LNB[:, k, 512:1024].bitcast(BF16)[:, i * 512:(i + 1) * 512] for k in range(3) for i in range(2)]
        KT2 = PTR[7:9]
        PTR = PTR[0:7]
        kt2i = [0]
        VH = bv(A1, 0, NT * 256).rearrange("p (t c) -> p t c", c=256)
        ptc = [0]
        otc = [0]
        SKEW = 4

        def attention(h, b, krows, scale, oT, chunks):
            qT, kT, va = QK[b], QK[2 + b], VA[b]
            steps = []
            gpar = {}
            for g in range(5):
                c0, n = gcols(g)
                gpar[g] = otc[0] % 2
                otc[0] += 1
                tl = [t for t in range(NT) if t <= GT[g][-1]]
                for j in tl:
                    steps.append((g, j, j == tl[0], j == tl[-1]))

            def qk(step):
                g, j, first, last = step
                c0, n = gcols(g)
                kr = trows(j)
                r = j - GT[g][0]
                off = 128 * r if r > 0 else 0
                diag = r >= 0
                w = n - off
                ST = nbank()
                ptb = PTR[ptc[0] % len(PTR)]
                ptc[0] += 1
                S.op("pe", "matmul", out=ST[0:kr, 0:w], lhsT=kT[0:krows, j * 128:j * 128 + kr], rhs=qT[0:krows, c0 + off:c0 + n], start=True, stop=not diag)
                if diag:
                    dw = min(128, w)
                    S.op("pe", "matmul", out=ST[0:kr, 0:dw], lhsT=identb[0:kr, 0:kr], rhs=maskb[0:kr, 0:dw], start=False, stop=True)
                S.op("act", "activation", out=ptb[0:kr, 0:w], in_=ST[0:kr, 0:w], func=AF.Exp, scale=scale)
                return (step, ptb, kr, off, w)

            def pv(info):
                (g, j, first, last), ptb, kr, off, w = info
                c0, n = gcols(g)
                OT = banks[6 + gpar[g]]
                S.op("pe", "matmul", out=OT[:, off:n], lhsT=va[0:kr, j, :], rhs=ptb[0:kr, 0:w], start=first, stop=last)
                if last:
                    sb_ = sbs[gpar[g]]
                    if h % 2 == 0:
                        o_lo, s_lo = 0, 64
                    else:
                        o_lo, s_lo = 64, 0
                    S.op("act", "activation", out=sb_[o_lo:o_lo + 64, 0:n], in_=OT[s_lo:s_lo + 64, 0:n], func=AF.Ln)
                    S.op("act", "activation", out=sb_[o_lo:o_lo + 64, 0:n], in_=sb_[o_lo:o_lo + 64, 0:n], func=AF.Exp, scale=-1.0)
                    S.op("dve", "tensor_tensor", out=oT[o_lo:o_lo + 64, h // 2, c0:c0 + n], in0=OT[o_lo:o_lo + 64, 0:n], in1=sb_[o_lo:o_lo + 64, 0:n], op=ALU.mult)

            infos = []
            for idx, st in enumerate(steps):
                infos.append(qk(st))
                if idx >= SKEW:
                    pv(infos[idx - SKEW])
                if idx % 5 == 4 and chunks:
                    chunks.pop(0)()
            for idx in range(max(0, len(steps) - SKEW), len(steps)):
                pv(infos[idx])
            while chunks:
                chunks.pop(0)()

        def v_evac(PV, tiles, va, h):
            off = 0 if h % 2 == 0 else 64
            full = [t for t in tiles if trows(t) == 128]
            if full:
                S.op("dve", "tensor_copy", out=va[:, full[0]:full[0] + len(full), off:off + 64],
                     in_=PV[:, 0:64 * len(full)].rearrange("p (t c) -> p t c", c=64))
            if len(full) < len(tiles):
                i = len(full)
                S.op("dve", "tensor_copy", out=va[0:16, tiles[i], off:off + 64], in_=PV[0:16, 64 * i:64 * (i + 1)])

        RTE = [bv(A2, 10240, 512, F32), bv(A2, 11264, 512, F32)]

        def mla_chunks(h, b, early=False):
            qT, kT, va = QK[b], QK[2 + b], VA[b]

            def qpart(g):
                c0, n = gcols(g)
                PA, PBq = banks[4], banks[5]
                for kc in range(3):
                    S.op("pe", "matmul", out=PA[0:96, 0:n], lhsT=wq[:, kc, h * 96:(h + 1) * 96], rhs=latT[:, kc, c0:c0 + n], start=(kc == 0), stop=(kc == 2))
                for kc in range(3):
                    S.op("pe", "matmul", out=PBq[0:96, 0:n], lhsT=wqB[:, kc, h, :], rhs=latT[:, kc, c0:c0 + n], start=(kc == 0), stop=(kc == 2))
                S.op("dve", "tensor_copy", out=qT[0:64, c0:c0 + n], in_=PA[0:64, 0:n])
                if early:
                    rope(PA, PBq, qT, c0, n, RTE[0], RTE[1])
                else:
                    rope(PA, PBq, qT, c0, n)

            def kvpart(g):
                c0, n = gcols(g)
                PK, PV = banks[4], banks[5]
                S.op("pe", "matmul", out=PK[0:64, 0:n], lhsT=wkv[:, h * 128:h * 128 + 64], rhs=latT[:, 3, c0:c0 + n], start=True, stop=True)
                S.op("dve", "tensor_copy", out=kT[0:64, c0:c0 + n], in_=PK[0:64, 0:n])
                S.op("pool", "tensor_copy", out=kT[64:96, c0:c0 + n], in_=kpeT[64:96, c0:c0 + n])
                for i, t in enumerate(GT[g]):
                    rows = trows(t)
                    S.op("pe", "matmul", out=PV[0:rows, 64 * i:64 * (i + 1)], lhsT=latT[:, 3, t * 128:t * 128 + rows], rhs=wkv[:, h * 128 + 64:h * 128 + 128], start=True, stop=True)
                v_evac(PV, GT[g], va, h)
            parts = []
            for g in range(5):
                parts.append(lambda g=g: qpart(g))
                parts.append(lambda g=g: kvpart(g))
            return parts

        augk = [0]

        def fox_chunks(h, b):
            qT, kT, va = QK[b], QK[2 + b], VA[b]

            def qkpart(g):
                c0, n = gcols(g)
                if g == 0:
                    S.op("pool", "memset", qT[64:70, :], 1.0)
                    S.op("pool", "memset", kT[64:70, :], -1.0)
                    for i in range(3):
                        S.dma("sp", "aug%d" % (augk[0] % 6), qT[64 + i:65 + i, 0:L], cparts[h:h + 1, i, :]); augk[0] += 1
                        S.dma("sp", "aug%d" % (augk[0] % 6), kT[67 + i:68 + i, 0:L], cparts[h:h + 1, i, :]); augk[0] += 1
                PA = banks[4]
                for kc in range(8):
                    S.op("pe", "matmul", out=PA[:, 0:n], lhsT=wfox[:, kc, h * 128:(h + 1) * 128], rhs=hT[:, kc, c0:c0 + n], start=(kc == 0), stop=(kc == 7))
                S.op("dve", "tensor_scalar", out=qT[0:64, c0:c0 + n], in0=PA[0:64, 0:n], scalar1=0.125, scalar2=None, op0=ALU.mult)
                kt = KT2[kt2i[0] % 2]
                kt2i[0] += 1
                S.op("dve", "tensor_copy", out=kt[64:128, 0:n], in_=PA[64:128, 0:n])
                S.dma("sp", "ksh%d" % (kt2i[0] % 4), kT[0:64, c0:c0 + n], kt[64:128, 0:n])

            def vpart(g):
                if h % 4 == 0:
                    tl = list(GT[g])
                    for a in range(0, len(tl), 2):
                        PV = banks[5]
                        pair = tl[a:a + 2]
                        for i, t in enumerate(pair):
                            rows = trows(t)
                            for kc in range(8):
                                S.op("pe", "matmul", out=PV[0:rows, 256 * i:256 * (i + 1)], lhsT=hT[:, kc, t * 128:t * 128 + rows],
                                     rhs=wfox[:, kc, 1024 + h * 64:1024 + h * 64 + 256], start=(kc == 0), stop=(kc == 7))
                        full = [t for t in pair if trows(t) == 128]
                        if full:
                            S.op("dve", "tensor_copy", out=VH[:, full[0]:full[0] + len(full), :],
                                 in_=PV[:, 0:256 * len(full)].rearrange("p (t c) -> p t c", c=256))
                        if len(full) < len(pair):
                            i = len(full)
                            S.op("dve", "tensor_copy", out=VH[0:16, pair[i], :], in_=PV[0:16, 256 * i:256 * (i + 1)])
                off = 0 if h % 2 == 0 else 64
                hq = (h % 4) * 64
                tl = list(GT[g])
                full = [t for t in tl if trows(t) == 128]
                S.op("pool", "tensor_copy", out=va[:, full[0]:full[0] + len(full), off:off + 64], in_=VH[:, full[0]:full[0] + len(full), hq:hq + 64])
                if len(full) < len(tl):
                    S.op("pool", "tensor_copy", out=va[0:16, tl[-1], off:off + 64], in_=VH[0:16, tl[-1], hq:hq + 64])
            parts = []
            for g in range(5):
                parts.append(lambda g=g: qkpart(g))
                parts.append(lambda g=g: vpart(g))
            return parts

        pc_blocks = []
        for rb in range(8):
            for c_ in (0, 1024):
                pc_blocks.append((scr_wg[rb * 128:(rb + 1) * 128, c_:c_ + 1024], w_in[rb * 128:(rb + 1) * 128, O_GATE + c_:O_GATE + c_ + 1024]))
        for rb in range(4):
            pc_blocks.append((scr_wb[rb * 128:(rb + 1) * 128, :], w_bm[rb * 128:(rb + 1) * 128, :]))
        for rb in range(4):
            pc_blocks.append((scr_wb[512 + rb * 128:512 + (rb + 1) * 128, :], w_bf[rb * 128:(rb + 1) * 128, :]))
        for rb in range(8):
            pc_blocks.append((scr_wo[rb * 128:(rb + 1) * 128, :], w_out[rb * 128:(rb + 1) * 128, :]))
        for rb in range(8):
            for c_ in range(0, 2 * DFF, 1024):
                w_ = min(1024, 2 * DFF - c_)
                pc_blocks.append((scr_wu[rb * 128:(rb + 1) * 128, c_:c_ + w_], w_up[rb * 128:(rb + 1) * 128, c_:c_ + w_]))
        PC_PER = 7
        pcn = [0]

        def precast_chunk(ci):
            for (dst_, src_) in pc_blocks[ci * PC_PER:(ci + 1) * PC_PER]:
                S.dma("pool", "pc%d" % (pcn[0] % 8), dst_, src_)
                pcn[0] += 1

        heads = [("m", h) for h in range(8)] + [("f", h) for h in range(8)]

        def chunks_for(i):
            kind, h = heads[i]
            return mla_chunks(h, i % 2) if kind == "m" else fox_chunks(h, i % 2)

        for b_ in range(2):
            S.op("pool", "memset", VA[b_][:, :, :], 1.0)
        p0 = mla_chunks(0, 0, early=True)
        for k_, g_ in ((17, 0), (18, 1), (20, 2), (23, 3)):
            drain_parts[k_] = [p0[2 * g_], p0[2 * g_ + 1]]
        run_AB()
        p0[8]()
        p0[9]()
        for i, (kind, h) in enumerate(heads):
            nxt = chunks_for(i + 1) if i + 1 < len(heads) else []
            if i == 1:
                nxt.append(scan_chunk)
            if i == 0:
                nxt.insert(2, wfox_chunk)
            if 2 <= i <= 13:
                nxt.insert(5, (lambda i=i: precast_chunk(i - 2)))
            if kind == "m":
                attention(h, i % 2, 96, 96.0 ** -0.5, omT, nxt)
            else:
                attention(h, i % 2, 70, 1.0, ofT, nxt)

        wgate = bv(A1, 0, 8 * 2048).rearrange("p (k n) -> p k n", k=8)
        wbm = bv(A1, 16384, 4 * 1024).rearrange("p (k n) -> p k n", k=4)
        wbf = bv(A1, 20480, 4 * 1024).rearrange("p (k n) -> p k n", k=4)
        wout = bv(A1, 24576, 8 * 1024).rearrange("p (k n) -> p k n", k=8)
        mergedT = bv(A1, 32768, 8 * 512).rearrange("p (c t) -> p c t", c=8)
        EB = [[bv(MISC, (i * 4 + k) * 1024, 512, F32) for k in range(4)] for i in range(2)]
        for pi, c_ in enumerate((0, 1024, 512, 1536)):
            S.dma("pool", "wg_%d" % pi, wgate[:, :, c_:c_ + 512], scr_wg[:, c_:c_ + 512].rearrange("(kc p) n -> p kc n", p=128))
        S.dma("pool", "wbm_0", wbm[:, :, :], scr_wb[0:512, :].rearrange("(kc p) n -> p kc n", p=128))
        S.dma("pool", "wbf_0", wbf[:, :, :], scr_wb[512:1024, :].rearrange("(kc p) n -> p kc n", p=128))
        S.dma("pool", "wo_0", wout[:, :, :], scr_wo.rearrange("(kc p) n -> p kc n", p=128))
        load_ln_params(ln_mix_g, ln_mix_b, "mix")
        deferred = []

        def c_iter(g, c):
            c0, n = gcols(g)
            sg1, sg2, m1, m2 = EB[c % 2]
            PG1, PG2, PB1, PB2 = banks[0], banks[1], banks[2], banks[3]
            for kc in range(8):
                S.op("pe", "matmul", out=PG1[:, 0:n], lhsT=wgate[:, kc, c * 128:(c + 1) * 128], rhs=hT[:, kc, c0:c0 + n], start=(kc == 0), stop=(kc == 7))
            S.op("act", "activation", out=sg1[:, 0:n], in_=PG1[:, 0:n], func=AF.Sigmoid, bias=BG(c))
            for kc in range(8):
                S.op("pe", "matmul", out=PG2[:, 0:n], lhsT=wgate[:, kc, 1024 + c * 128:1024 + (c + 1) * 128], rhs=hT[:, kc, c0:c0 + n], start=(kc == 0), stop=(kc == 7))
            S.op("act", "activation", out=sg2[:, 0:n], in_=PG2[:, 0:n], func=AF.Sigmoid, bias=BG(8 + c))
            for kc in range(4):
                S.op("pe", "matmul", out=PB1[:, 0:n], lhsT=wbm[:, kc, c * 128:(c + 1) * 128], rhs=omT[:, kc, c0:c0 + n], start=(kc == 0), stop=(kc == 3))
            S.op("dve", "tensor_tensor", out=m1[:, 0:n], in0=PB1[:, 0:n], in1=sg1[:, 0:n], op=ALU.mult)
            for kc in range(4):
                S.op("pe", "matmul", out=PB2[:, 0:n], lhsT=wbf[:, kc, c * 128:(c + 1) * 128], rhs=ofT[:, kc, c0:c0 + n], start=(kc == 0), stop=(kc == 3))
            S.op("dve", "tensor_tensor", out=m2[:, 0:n], in0=PB2[:, 0:n], in1=sg2[:, 0:n], op=ALU.mult)
            S.op("pool", "tensor_tensor", out=mergedT[:, c, 0:n], in0=m1[:, 0:n], in1=m2[:, 0:n], op=ALU.add)

        c_iters = [[(lambda g=g, c=c: c_iter(g, c)) for c in range(8)] for g in range(5)]
        stE = {}

        def e1(it):
            ti, t = it
            rows = trows(t)
            hb = LNB[:, t % 4, :]
            for cg in range(2):
                PM = banks[4 + cg]
                for c in range(8):
                    S.op("pe", "matmul", out=PM[0:rows, :], lhsT=mergedT[:, c, ti * 128:ti * 128 + rows], rhs=wout[:, c, cg * 512:(cg + 1) * 512], start=(c == 0), stop=(c == 7))
                S.op("dve", "scalar_tensor_tensor", out=hb[0:rows, cg * 512:(cg + 1) * 512], in0=hb[0:rows, cg * 512:(cg + 1) * 512], scalar=ALPHA, in1=PM[0:rows, :], op0=ALU.mult, op1=ALU.add)

        def e2(it):
            stE[it[1]] = ln_s1(LNB[:, it[1] % 4, :], trows(it[1]))

        def e3(it):
            ln_s2(LNB[:, it[1] % 4, :], trows(it[1]), stE[it[1]])

        def e4(it):
            ln_s3(LNB[:, it[1] % 4, :], trows(it[1]))

        def e5(it):
            t = it[1]
            rows = trows(t)
            tb = LNB[:, t % 4, :]
            while len(deferred) > 1:
                deferred.pop(0)()
            hi = ln_s4(tb, rows)
            S.dma("sp", "hs%d" % (t % 4), scr_h2[t * 128:t * 128 + rows, :], tb[0:rows, :])
            deferred.append(lambda t=t, rows=rows, hi=hi: transpose_to_hT(t, rows, hi))

        for g in range(5):
            for t in GT[g]:
                S.dma("sp", "hr%d" % (t % 4), LNB[0:trows(t), t % 4, :], scr_h[t * 128:t * 128 + trows(t), :])
            first = True
            while c_iters[g]:
                c_iters[g].pop(0)()
                if first:
                    first = False
                    while deferred:
                        deferred.pop(0)()
            items = list(enumerate(GT[g]))
            for it in items:
                e1(it)
            stages = [e2, e3, e4, e5]
            n_ = len(items)
            for k in range(n_ + len(stages) - 1):
                for si in range(len(stages) - 1, -1, -1):
                    i_ = k - si
                    if 0 <= i_ < n_:
                        stages[si](items[i_])
                if g + 1 < 5 and len(c_iters[g + 1]) > 1:
                    c_iters[g + 1].pop(0)()


        WU = [bv(A1, i * 4096, 8 * 512).rearrange("p (k n) -> p k n", k=8) for i in range(4)] + [bv(MISC, i * 4096, 8 * 512).rearrange("p (k n) -> p k n", k=8) for i in range(2)]
        actT = bv(A1, 16384, NJ * 512).rearrange("p (j t) -> p j t", j=NJ)
        GO = 30
        Gb = [bv(A1, 27648 + i * 1152, 548, F32) for i in range(2)]
        cvb = [bv(A1, 29952 + i * 1024, 512, F32) for i in range(2)]
        slb = [bv(A1, 32000 + i * 1024, 512, F32) for i in range(2)]
        carry = bv(A1, 34048, NJ * 2, F32).rearrange("p (j c) -> p j c", c=2)
        wdn = bv(A2, 0, NJ * 1024).rearrange("p (j n) -> p j n", j=NJ)
        S.op("dve", "memset", carry[:, :, :], 0.0)
        load_ln_params(ln_ffn_g, ln_ffn_b, "ffn")
        rounds = [(g, j) for g in range(5) for j in range(0, NJ, 4)]

        def issue_round(ri):
            g_, j_ = rounds[ri]
            nj = min(4, NJ - j_)
            rs = (ri % 3) * 2
            S.dma("pool", "wug%d" % (ri % 3), WU[rs][:, :, 0:nj * 128], scr_wu[:, j_ * 128:(j_ + nj) * 128].rearrange("(kc p) n -> p kc n", p=128))
            S.dma("pool", "wuv%d" % (ri % 3), WU[rs + 1][:, :, 0:nj * 128], scr_wu[:, DFF + j_ * 128:DFF + (j_ + nj) * 128].rearrange("(kc p) n -> p kc n", p=128))

        issue_round(0)
        issue_round(1)
        ri = -1
        for g in range(5):
            c0, n = gcols(g)
            for t in GT[g]:
                S.dma("sp", "hr%d" % (t % 4), LNB[0:trows(t), t % 4, :], scr_h2[t * 128:t * 128 + trows(t), :])
            pend_mult = None
            pend_silu = None
            for j in range(NJ):
                jj = j % 4
                if jj == 0:
                    ri += 1
                    if ri + 2 < len(rounds):
                        issue_round(ri + 2)
                    wg_, wv_ = WU[(ri % 3) * 2], WU[(ri % 3) * 2 + 1]
                    if g == 0 and j == 4:
                        while deferred:
                            deferred.pop(0)()
                        S.dma("pool", "wd0", wdn[:, :, :], w_dn.rearrange("(j p) n -> p j n", p=128))
                pj = j % 2
                PUg, PUv = banks[(j % 3) * 2], banks[(j % 3) * 2 + 1]
                for kc in range(8):
                    S.op("pe", "matmul", out=PUg[:, 0:n], lhsT=wg_[:, kc, jj * 128:(jj + 1) * 128], rhs=hT[:, kc, c0:c0 + n], start=(kc == 0), stop=(kc == 7))
                for kc in range(8):
                    S.op("pe", "matmul", out=PUv[:, 0:n], lhsT=wv_[:, kc, jj * 128:(jj + 1) * 128], rhs=hT[:, kc, c0:c0 + n], start=(kc == 0), stop=(kc == 7))
                G_ = Gb[pj]
                cv = cvb[pj]
                sl = slb[pj]
                S.op("act", "activation", out=G_[:, GO + 2:GO + 2 + n], in_=PUg[:, 0:n], func=AF.Copy)
                S.op("act", "activation", out=G_[:, GO:GO + 2], in_=carry[:, j, :], func=AF.Copy)
                if pend_silu is not None:
                    pend_silu()
                S.op("dve", "tensor_scalar", out=cv[:, 0:n], in0=G_[:, GO + 2:GO + 2 + n], scalar1=CW(2, j), scalar2=CB(j), op0=ALU.mult, op1=ALU.add)
                S.op("dve", "scalar_tensor_tensor", out=cv[:, 0:n], in0=G_[:, GO + 1:GO + 1 + n], scalar=CW(1, j), in1=cv[:, 0:n], op0=ALU.mult, op1=ALU.add)
                S.op("dve", "scalar_tensor_tensor", out=cv[:, 0:n], in0=G_[:, GO:GO + n], scalar=CW(0, j), in1=cv[:, 0:n], op0=ALU.mult, op1=ALU.add)
                S.op("act", "activation", out=carry[:, j, :], in_=G_[:, GO + n:GO + n + 2], func=AF.Copy)
                if pend_mult is not None:
                    pend_mult()
                pend_silu = (lambda cv=cv, sl=sl: S.op("act", "activation", out=sl[:, 0:n], in_=cv[:, 0:n], func=AF.Silu))
                pend_mult = (lambda j=j, PUv=PUv, sl=sl: S.op("dve", "tensor_tensor", out=actT[:, j, 0:n], in0=PUv[:, 0:n], in1=sl[:, 0:n], op=ALU.mult))
            pend_silu()
            pend_mult()
            stF = {}

            def f1(it):
                ti, t = it
                rows = trows(t)
                hb = LNB[:, t % 4, :]
                for cg in range(2):
                    PD = banks[4 + (ti % 2) * 2 + cg]
                    for j in range(NJ):
                        S.op("pe", "matmul", out=PD[0:rows, :], lhsT=actT[:, j, ti * 128:ti * 128 + rows], rhs=wdn[:, j, cg * 512:(cg + 1) * 512], start=(j == 0), stop=(j == NJ - 1))
                    S.op("dve", "scalar_tensor_tensor", out=hb[0:rows, cg * 512:(cg + 1) * 512], in0=hb[0:rows, cg * 512:(cg + 1) * 512], scalar=ALPHA, in1=PD[0:rows, :], op0=ALU.mult, op1=ALU.add)

            def f2(it):
                stF[it[1]] = ln_s1(LNB[:, it[1] % 4, :], trows(it[1]))

            def f3(it):
                ln_s2(LNB[:, it[1] % 4, :], trows(it[1]), stF[it[1]])

            def f4(it):
                ln_s3(LNB[:, it[1] % 4, :], trows(it[1]))

            def f5(it):
                t = it[1]
                rows = trows(t)
                ob = LNB[:, t % 4, :]
                ln_s4(ob, rows, want16=False)
                if t == 0:
                    S.dma("sp", "o%d" % (t % 4), out[0:112, :], ob[16:128, :])
                else:
                    S.dma("sp", "o%d" % (t % 4), out[t * 128 - 16:t * 128 - 16 + rows, :], ob[0:rows, :])

            run_skewed([f1, f2, f3, f4, f5], list(enumerate(GT[g])))

        S.emit(final_dma_keys=["o0", "o1", "o2", "o3"])
    return nc


_CACHE = {}


def _consts():
    ident = np.eye(128, dtype=np.float32)
    k = np.arange(128)[:, None]
    q = np.arange(128)[None, :]
    mask = np.where(q >= k, 0.0, -30000.0).astype(np.float32)
    half = 16
    inv_freq = (np.float32(10000.0) ** (-np.arange(half, dtype=np.float32) / np.float32(half))).astype(np.float32)
    pos = np.arange(L, dtype=np.float32)
    ang = (pos[None, :] * inv_freq[:, None]).astype(np.float32)
    cos = np.cos(ang).astype(np.float32)
    sin = np.sin(ang).astype(np.float32)
    c_cos = np.concatenate([cos, cos], axis=0)
    c_sin = np.concatenate([-sin, sin], axis=0)
    return ident, mask, np.ascontiguousarray(c_cos), np.ascontiguousarray(c_sin)


def kernel(**inputs):
    if "nc" not in _CACHE:
        _CACHE["nc"] = build_program()
    nc = _CACHE["nc"]
    f = lambda a: np.ascontiguousarray(np.asarray(a, dtype=np.float32))
    ident, mask, c_cos, c_sin = _consts()
    shared = {
        "meta_tokens": f(inputs["meta_tokens"]),
        "ln_emb_g": f(inputs["ln_emb_g"]), "ln_emb_b": f(inputs["ln_emb_b"]),
        "w_in": f(inputs["w_in"])[0], "b_gate": f(inputs["b_gate"])[0], "b_forget": f(inputs["b_forget"])[0],
        "q_norm_g": f(inputs["q_norm_g"])[0], "w_q_up": f(inputs["w_q_up"])[0],
        "kv_norm_g": f(inputs["kv_norm_g"])[0], "w_kv_up": f(inputs["w_kv_up"])[0],
        "w_branch_mla": f(inputs["w_branch_mla"])[0], "w_branch_fox": f(inputs["w_branch_fox"])[0],
        "w_out": f(inputs["w_out"])[0],
        "ln_mix_g": f(inputs["ln_mix_g"])[0], "ln_mix_b": f(inputs["ln_mix_b"])[0],
        "w_ffn_up": f(inputs["w_ffn_up"])[0], "conv_w": f(inputs["conv_w"])[0], "conv_b": f(inputs["conv_b"])[0],
        "w_ffn_down": f(inputs["w_ffn_down"])[0],
        "ln_ffn_g": f(inputs["ln_ffn_g"])[0], "ln_ffn_b": f(inputs["ln_ffn_b"])[0],
        "c_ident": ident, "c_mask": mask, "c_cos": c_cos, "c_sin": c_sin,
    }
    x = f(inputs["x"])
    in_maps = []
    for b in range(8):
        m = dict(shared)
        m["x"] = x[b]
        in_maps.append(m)
    res = run_bass_kernel_spmd(nc, in_maps, core_ids=list(range(8)))
    return np.stack([np.asarray(r["out"], dtype=np.float32) for r in res.results], axis=0)
```
